# Optimizing a Trainium2 kernel written in Bass

```python
import jax, jax.numpy as jnp
from jax import lax
import numpy as np

D_MODEL = 4096
BATCH = 8
SEQ = 2048
DEPTH = 4
DEC_BATCH = 2
DEC_SEQ = 4096
PAST_LEN = 128

GRID_W = 64
BRANCH_W = D_MODEL // 2
N_BRANCH = 3
SGU_CHUNK = 128
SGU_GROUPS = 8
GLA_HEADS = 4
GLA_KEY_W = BRANCH_W // 2
GLA_HEAD_K = GLA_KEY_W // GLA_HEADS
GLA_HEAD_V = BRANCH_W // GLA_HEADS
GLA_LOW_RANK = 16
GLA_GATE_TEMP = 16.0
GLA_CHUNK = 64
NA_HEAD_DIM = 128
NA_HEADS = BRANCH_W // NA_HEAD_DIM
NA_ROWS_MAX = 8
NA_COLS = 16
EPS = 1e-6

IN_SPLIT_SIZES = (
    BRANCH_W, BRANCH_W, BRANCH_W,
    GLA_KEY_W, GLA_KEY_W, BRANCH_W, BRANCH_W,
    GLA_LOW_RANK, GLA_LOW_RANK,
    BRANCH_W, BRANCH_W, BRANCH_W, BRANCH_W,
)
IN_COLS = sum(IN_SPLIT_SIZES)

kernel_name = "hybrid_sgu_gla_natten_encoder"


def rmsnorm(x, g):
    xf = x.astype(jnp.float32)
    y = xf * lax.rsqrt(jnp.mean(xf * xf, axis=-1, keepdims=True) + EPS)
    return (y * g.astype(jnp.float32)).astype(x.dtype)


def layernorm(x, g):
    xf = x.astype(jnp.float32)
    mu = jnp.mean(xf, axis=-1, keepdims=True)
    xc = xf - mu
    y = xc * lax.rsqrt(jnp.mean(xc * xc, axis=-1, keepdims=True) + EPS)
    return (y * g.astype(jnp.float32)).astype(x.dtype)


def sgu_mixer(u, v, ln_g, w_s, b_s):
    B, T, _ = u.shape
    nc = T // SGU_CHUNK
    vn = layernorm(v, ln_g).reshape(B, nc, SGU_CHUNK, SGU_GROUPS, BRANCH_W // SGU_GROUPS)
    s = jnp.einsum('gij,bcjgd->bcigd', w_s, vn) + b_s.T[None, None, :, :, None]
    return u * s.reshape(B, T, BRANCH_W)


def _to_chunks(t, c):
    B, T, H, d = t.shape
    return t.reshape(B, T // c, c, H, d).transpose(1, 0, 3, 2, 4)


def gla_causal(q, k, v, g):
    B, T, H, dk = q.shape
    dv = v.shape[-1]
    f32 = jnp.float32
    qc, kc, vc, gc = [_to_chunks(t.astype(f32), GLA_CHUNK) for t in (q, k, v, g)]
    Gc = jnp.cumsum(gc, axis=-2)
    mask = jnp.tril(jnp.ones((GLA_CHUNK, GLA_CHUNK), dtype=bool))

    def step(S, inp):
        qi, ki, vi, Gi = inp
        Gl = Gi[..., -1:, :]
        q_e = qi * jnp.exp(Gi)
        k_e = ki * jnp.exp(-Gi)
        a = jnp.where(mask, jnp.einsum('bhid,bhjd->bhij', q_e, k_e), 0.0)
        o = jnp.einsum('bhij,bhjv->bhiv', a, vi) + jnp.einsum('bhid,bhdv->bhiv', q_e, S)
        S = jnp.exp(Gl[..., 0, :])[..., None] * S + jnp.einsum(
            'bhjd,bhjv->bhdv', ki * jnp.exp(Gl - Gi), vi)
        return S, o

    S0 = jnp.zeros((B, H, dk, dv), f32)
    _, o = lax.scan(step, S0, (qc, kc, vc, Gc))
    return o.transpose(1, 0, 3, 2, 4).reshape(B, T, H, dv).astype(v.dtype)


def gla_mixer(xq, xk, xv, lr_f, lr_b, w_gk, b_gk, norm_g):
    B, T, _ = xq.shape
    q = xq.reshape(B, T, GLA_HEADS, GLA_HEAD_K) * (GLA_HEAD_K ** -0.5)
    k = xk.reshape(B, T, GLA_HEADS, GLA_HEAD_K)
    v = xv.reshape(B, T, GLA_HEADS, GLA_HEAD_V)
    g_f = (jax.nn.log_sigmoid((lr_f @ w_gk[0] + b_gk[0]).astype(jnp.float32)) / GLA_GATE_TEMP
           ).reshape(B, T, GLA_HEADS, GLA_HEAD_K)
    g_b = (jax.nn.log_sigmoid((lr_b @ w_gk[1] + b_gk[1]).astype(jnp.float32)) / GLA_GATE_TEMP
           ).reshape(B, T, GLA_HEADS, GLA_HEAD_K)
    o_f = gla_causal(q, k, v, g_f)
    o_b = jnp.flip(gla_causal(jnp.flip(q, 1), jnp.flip(k, 1), jnp.flip(v, 1), jnp.flip(g_b, 1)), 1)
    o = rmsnorm(o_f + o_b, norm_g)
    return o.reshape(B, T, BRANCH_W)


def na_mixer(xq, xk, xv, rpb):
    B, T, _ = xq.shape
    rows = T // GRID_W
    kr = min(NA_ROWS_MAX, rows)
    q = xq.reshape(B, T, NA_HEADS, NA_HEAD_DIM) * (NA_HEAD_DIM ** -0.5)
    k = xk.reshape(B, T, NA_HEADS, NA_HEAD_DIM)
    v = xv.reshape(B, T, NA_HEADS, NA_HEAD_DIM)
    cols = jnp.arange(GRID_W)
    cs = jnp.clip(cols - NA_COLS // 2, 0, GRID_W - NA_COLS)
    col_ok = (cols[None, :] >= cs[:, None]) & (cols[None, :] < cs[:, None] + NA_COLS)
    dc_idx = jnp.clip(cols[None, :] - cols[:, None] + NA_COLS - 1, 0, 2 * NA_COLS - 2)
    neg = jnp.finfo(jnp.float32).min

    def row_block(r):
        rs = jnp.clip(r - kr // 2, 0, rows - kr)
        qr = lax.dynamic_slice_in_dim(q, r * GRID_W, GRID_W, axis=1)
        kb = lax.dynamic_slice_in_dim(k, rs * GRID_W, kr * GRID_W, axis=1
                                      ).reshape(B, kr, GRID_W, NA_HEADS, NA_HEAD_DIM)
        vb = lax.dynamic_slice_in_dim(v, rs * GRID_W, kr * GRID_W, axis=1
                                      ).reshape(B, kr, GRID_W, NA_HEADS, NA_HEAD_DIM)
        dr_idx = rs + jnp.arange(kr) - r + NA_ROWS_MAX - 1
        bias = rpb[:, dr_idx[:, None, None], dc_idx[None, :, :]]
        s = jnp.einsum('bqhd,bnkhd->bhqnk', qr, kb).astype(jnp.float32)
        s = s + bias.transpose(0, 2, 1, 3)[None].astype(jnp.float32)
        s = jnp.where(col_ok[None, None, :, None, :], s, neg)
        p = jax.nn.softmax(s.reshape(B, NA_HEADS, GRID_W, kr * GRID_W), axis=-1)
        p = p.reshape(B, NA_HEADS, GRID_W, kr, GRID_W).astype(v.dtype)
        return jnp.einsum('bhqnk,bnkhd->bqhd', p, vb)

    o = lax.map(row_block, jnp.arange(rows))
    return o.transpose(1, 0, 2, 3, 4).reshape(B, T, BRANCH_W)


def layer(x, norm_g, w_in, sgu_ln_g, sgu_w, sgu_b, gla_w_gk, gla_b_gk, gla_norm_g,
          na_rpb, w_gate, b_gate, w_branch, w_out):
    B, T, _ = x.shape
    h = rmsnorm(x, norm_g)
    split_idx = tuple(int(i) for i in np.cumsum(IN_SPLIT_SIZES)[:-1])
    (a_u, a_v, a_z, b_q, b_k, b_v, b_z, b_lf, b_lb,
     c_q, c_k, c_v, c_z) = jnp.split(h @ w_in, split_idx, axis=-1)
    y_a = sgu_mixer(a_u, a_v, sgu_ln_g, sgu_w, sgu_b) * jax.nn.silu(a_z)
    y_b = gla_mixer(b_q, b_k, b_v, b_lf, b_lb, gla_w_gk, gla_b_gk, gla_norm_g) * jax.nn.silu(b_z)
    y_c = na_mixer(c_q, c_k, c_v, na_rpb) * jax.nn.silu(c_z)
    ys = jnp.stack([y_a, y_b, y_c], axis=2)
    branch = jnp.einsum('btnw,nwd->btnd', ys, w_branch)
    gates = jax.nn.sigmoid(h @ w_gate + b_gate).reshape(B, T, N_BRANCH, D_MODEL)
    merged = jnp.sum(gates * branch, axis=2)
    return x + merged @ w_out


def trunk(x, norm_g, w_in, sgu_ln_g, sgu_w, sgu_b, gla_w_gk, gla_b_gk, gla_norm_g,
          na_rpb, w_gate, b_gate, w_branch, w_out, final_norm_g):
    for l in range(DEPTH):
        x = layer(x, norm_g[l], w_in[l], sgu_ln_g[l], sgu_w[l], sgu_b[l], gla_w_gk[l],
                  gla_b_gk[l], gla_norm_g[l], na_rpb[l], w_gate[l], b_gate[l],
                  w_branch[l], w_out[l])
    return rmsnorm(x, final_norm_g)


def setup_inputs(seed: int = 0) -> dict:
    key = jax.random.key(seed)
    ks = jax.random.split(key, 16)
    n = jax.random.normal
    f32 = jnp.float32
    return {
        "x_prompt": n(ks[0], (BATCH, SEQ, D_MODEL), f32),
        "x_sample": n(ks[1], (DEC_BATCH, DEC_SEQ, D_MODEL), f32),
        "norm_g": 1.0 + 0.05 * n(ks[2], (DEPTH, D_MODEL), f32),
        "w_in": n(ks[3], (DEPTH, D_MODEL, IN_COLS), f32) * (D_MODEL ** -0.5),
        "sgu_ln_g": 1.0 + 0.05 * n(ks[4], (DEPTH, BRANCH_W), f32),
        "sgu_w": n(ks[5], (DEPTH, SGU_GROUPS, SGU_CHUNK, SGU_CHUNK), f32) * (SGU_CHUNK ** -0.5),
        "sgu_b": 1.0 + 0.1 * n(ks[6], (DEPTH, SGU_GROUPS, SGU_CHUNK), f32),
        "gla_w_gk": n(ks[7], (DEPTH, 2, GLA_LOW_RANK, GLA_KEY_W), f32) * (GLA_LOW_RANK ** -0.5),
        "gla_b_gk": 0.1 * n(ks[8], (DEPTH, 2, GLA_KEY_W), f32),
        "gla_norm_g": 1.0 + 0.05 * n(ks[9], (DEPTH, GLA_HEAD_V), f32),
        "na_rpb": 0.1 * n(ks[10], (DEPTH, NA_HEADS, 2 * NA_ROWS_MAX - 1, 2 * NA_COLS - 1), f32),
        "w_gate": n(ks[11], (DEPTH, D_MODEL, N_BRANCH * D_MODEL), f32) * (D_MODEL ** -0.5),
        "b_gate": 0.1 * n(ks[12], (DEPTH, N_BRANCH * D_MODEL), f32),
        "w_branch": n(ks[13], (DEPTH, N_BRANCH, BRANCH_W, D_MODEL), f32) * (BRANCH_W ** -0.5),
        "w_out": n(ks[14], (DEPTH, D_MODEL, D_MODEL), f32) * (D_MODEL ** -0.5),
        "final_norm_g": 1.0 + 0.05 * n(ks[15], (D_MODEL,), f32),
    }


def reference(x_prompt, x_sample, norm_g, w_in, sgu_ln_g, sgu_w, sgu_b, gla_w_gk, gla_b_gk,
              gla_norm_g, na_rpb, w_gate, b_gate, w_branch, w_out, final_norm_g):
    y_prompt = trunk(x_prompt, norm_g, w_in, sgu_ln_g, sgu_w, sgu_b, gla_w_gk, gla_b_gk,
                     gla_norm_g, na_rpb, w_gate, b_gate, w_branch, w_out, final_norm_g)
    y_sample = trunk(x_sample, norm_g, w_in, sgu_ln_g, sgu_w, sgu_b, gla_w_gk, gla_b_gk,
                     gla_norm_g, na_rpb, w_gate, b_gate, w_branch, w_out, final_norm_g)
    return (y_prompt, y_sample)
```

```python
import numpy as np
import concourse.bass as bass
import concourse.mybir as mybir
from concourse.bass_utils import run_bass_kernel_spmd

F32 = mybir.dt.float32
BF16 = mybir.dt.bfloat16
AF = mybir.ActivationFunctionType
ALU = mybir.AluOpType
AX = mybir.AxisListType
ENG = ("pe", "act", "dve", "pool", "sp")

D = 4096
DEPTH = 4
NT = 4096
TT = 512
NTT = NT // TT
INC = 20512
BW = 2048
EPS = 1e-6
NEG = -30000.0

FM_AU, FM_ASZ, FM_BQ, FM_BK, FM_LR, FM_CQ, FM_CK, FM_CSZ, NFM = 0, 16, 32, 40, 48, 49, 65, 81, 97
TM_AV, TM_BK, TM_BV, TM_BSZ, TM_CV, NTM = 0, 2048, 3072, 5120, 7168, 9216


class Res:
    __slots__ = ("name", "lw", "rd", "sem", "cnt", "const")

    def __init__(self, name, const=False):
        self.name = name
        self.lw = None
        self.rd = {}
        self.sem = None
        self.cnt = 0
        self.const = const


class Op:
    __slots__ = ("eng", "fn", "deps", "sig", "val", "dres", "dval")

    def __init__(self, eng, fn):
        self.eng = eng
        self.fn = fn
        self.deps = []
        self.sig = False
        self.val = 0
        self.dres = None
        self.dval = 0


class Phase:
    def __init__(self, nc, name):
        self.nc = nc
        self.name = name
        self.ops = {e: [] for e in ENG}
        self.dres = []

    def _track(self, o, reads, writes):
        deps = o.deps
        for r in reads:
            if r.lw is not None:
                deps.append(r.lw)
        for r in writes:
            if r.lw is not None:
                deps.append(r.lw)
            deps.extend(r.rd.values())
        key = o.eng if o.dres is None else ("d", id(o.dres))
        for r in reads:
            if not r.const:
                r.rd[key] = o
        for r in writes:
            r.lw = o
            r.rd = {}
        for d in deps:
            d.sig = True
        self.ops[o.eng].append(o)
        return o

    def op(self, eng, fn, reads=(), writes=()):
        return self._track(Op(eng, fn), reads, writes)

    def dma(self, q, out, in_, key, reads=(), writes=()):
        o = Op(q, (out, in_))
        if key.sem is None:
            self.dres.append(key)
            key.sem = True
        key.cnt += 16
        o.dres = key
        o.dval = key.cnt
        return self._track(o, reads, writes)

    def emit(self):
        nc = self.nc
        esem = {e: nc.alloc_semaphore(f"{self.name}_{e}") for e in ENG}
        for i, r in enumerate(self.dres):
            r.sem = nc.alloc_semaphore(f"{self.name}_d{i}")
        for e in ENG:
            c = 0
            for o in self.ops[e]:
                if o.dres is None and o.sig:
                    c += 1
                    o.val = c
        lastq = {}
        for e in ENG:
            for o in self.ops[e]:
                if o.dres is not None:
                    lastq[id(o.dres)] = (e, o.dres)
        engobj = {"pe": "tensor", "act": "scalar", "dve": "vector", "pool": "gpsimd", "sp": "sync"}

        def run(e):
            def body(eng):
                waited = {}
                for o in self.ops[e]:
                    need = {}
                    for d in o.deps:
                        if d.dres is not None:
                            s, v = d.dres.sem, d.dval
                        else:
                            if d.eng == "pe" and e == "pe":
                                continue
                            s, v = esem[d.eng], d.val
                        k = id(s)
                        if waited.get(k, 0) >= v:
                            continue
                        if k not in need or need[k][1] < v:
                            need[k] = (s, v)
                    for k, (s, v) in need.items():
                        eng.wait_ge(s, v)
                        waited[k] = v
                    if o.dres is not None:
                        out, in_ = o.fn
                        eng.dma_start(out=out, in_=in_).then_inc(o.dres.sem, 16)
                    else:
                        ins = o.fn(eng)
                        if o.sig:
                            ins.then_inc(esem[e], 1)
                for k, (q, r) in lastq.items():
                    if q == e:
                        eng.wait_ge(r.sem, r.cnt)
            return body

        with nc.Block() as block:
            for e in ENG:
                if self.ops[e]:
                    getattr(block, engobj[e])(run(e))
        nc.all_engine_barrier()
        for r in self.dres:
            r.sem = None
            r.cnt = 0


class Buf:
    __slots__ = ("t", "r")

    def __init__(self, t, name, const=False):
        self.t = t
        self.r = Res(name, const)


class K:
    def __init__(self, depth=DEPTH, dbg=False):
        self.depth = depth
        self.dbg = dbg
        nc = self.nc = bass.Bass("TRN2", target_bir_lowering=False)
        L = depth

        def din(name, shape):
            return nc.dram_tensor(name, list(shape), F32, kind="ExternalInput").ap()

        self.x_in = din("x", [NT, D])
        self.carry = din("carry", [128, 1])
        self.w_in = din("w_in", [L, D, INC])
        self.w_gate = din("w_gate", [L, D, 3 * D])
        self.w_branch = din("w_branch", [L, 3, BW, D])
        self.w_out = din("w_out", [L, D, D])
        self.norm_gc = din("norm_gc", [L, 128, 32])
        self.fin_g = din("fin_g", [D])
        self.b_gatec = din("b_gatec", [L, 128, 96])
        self.sgu_lng = din("sgu_lng", [L, BW])
        self.sgu_wT = din("sgu_wT", [L, 128, 8, 128])
        self.sgu_b = din("sgu_b", [L, 1024])
        self.gla_wgk = din("gla_wgk", [L, 2, 17, 1024])
        self.gla_ng = din("gla_ng", [L, 512])
        self.na_bias = din("na_bias", [L, 3, 6, 16, 128, 256])
        self.y_out = nc.dram_tensor("y", [NT, D], F32, kind="ExternalOutput").ap()

        def scr(name, shape, dt):
            okind = "ExternalOutput" if (dbg and (dbg is True or name in dbg)) else "Internal"
            return nc.dram_tensor(name, list(shape), dt, kind=okind).ap()

        self.xs = scr("xs", [NT, D], F32)
        self.hT = scr("hT", [32, 128, NT], BF16)
        self.fms = scr("fms", [NFM, 128, NT], BF16)
        self.tms = scr("tms", [NT, NTM], BF16)
        self.yT = scr("yT", [48, 128, NT], BF16)
        self.og = scr("og", [2, NT, BW], F32)
        self.mT = scr("mT", [32, 128, NT], BF16)
        self.w_in_b = nc.dram_tensor("w_in_b", [D, INC], BF16).ap()
        self.wg_s = nc.dram_tensor("wg_s", [32, 3, 128, 32 * 128], BF16).ap()
        self.wb_s = nc.dram_tensor("wb_s", [32, 3, 128, 16 * 128], BF16).ap()
        self.w_out_b = nc.dram_tensor("w_out_b", [D, D], BF16).ap()
        self.nb_b = nc.dram_tensor("nb_b", [30, 16, 128, 256], BF16).ap()
        self.pid = 0

    def begin(self, name):
        self.pid += 1
        self.ph = Phase(self.nc, f"{name}{self.pid}")
        self._bn = 0
        return self.ph

    def sb(self, shape, dt, name=None, const=False):
        self._bn += 1
        nm = f"{self.ph.name}_{name or 'b'}{self._bn}"
        return Buf(self.nc.alloc_sbuf_tensor(nm, list(shape), dt), nm, const)

    def ps(self, shape=(128, 512), dt=F32, name=None):
        self._bn += 1
        nm = f"{self.ph.name}_{name or 'ps'}{self._bn}"
        return Buf(self.nc.alloc_psum_tensor(nm, list(shape), dt), nm)

    def consts(self, ph, need_ident=True):
        idf = self.sb([128, 128], F32, "idf")
        idb = self.sb([128, 128], BF16, "idb")
        ph.op("pool", lambda e: e.memset(idf.t[:], 0.0), writes=[idf.r])
        ph.op("pool", lambda e: e.affine_select(out=idf.t[:], in_=idf.t[:], pattern=[[-1, 128]],
                                                 compare_op=ALU.not_equal, fill=1.0, base=0,
                                                 channel_multiplier=1),
              reads=[idf.r], writes=[idf.r])
        ph.op("dve", lambda e: e.tensor_copy(out=idb.t[:], in_=idf.t[:]), reads=[idf.r], writes=[idb.r])
        idb.r.const = True
        return idf, idb

    def phase_convert(self, l):
        nc = self.nc
        with nc.cleanup_on_exit():
            ph = self.begin("cv")
            k = Res("cv")
            for i in range(32):
                ph.dma("pool", self.w_in_b[i * 128:(i + 1) * 128, :], self.w_in[l, i * 128:(i + 1) * 128, :], k)
            wg = self.w_gate[l].rearrange("(k p) (b f c) -> f b p k c", p=128, b=3, f=32)
            wb = self.w_branch[l].rearrange("b (k p) (f c) -> f b p k c", p=128, f=32)
            for f in range(32):
                for b in range(3):
                    ph.dma("pool", self.wg_s[f, b].rearrange("p (k c) -> p k c", k=32), wg[f, b], k)
                    ph.dma("pool", self.wb_s[f, b].rearrange("p (k c) -> p k c", k=16), wb[f, b], k)
            for i in range(16):
                ph.dma("pool", self.w_out_b[i * 256:(i + 1) * 256, :], self.w_out[l, i * 256:(i + 1) * 256, :], k)
            nb = self.na_bias[l].rearrange("a t h p q -> (a t) h p q")
            for i in range(18):
                ph.dma("pool", self.nb_b[i], nb[i], k)
            car = self.sb([128, 2], F32, "car", const=True)
            ph.dma("sp", car.t[:, 0:1], self.carry, car.r, writes=[car.r])
            ph.op("dve", lambda e: e.tensor_scalar(out=car.t[:, 1:2], in0=car.t[:, 0:1], scalar1=-1.0, scalar2=1.0,
                                                   op0=ALU.mult, op1=ALU.add), reads=[car.r], writes=[car.r])
            ti = [self.sb([128, 16, 256], F32, "ti") for _ in range(2)]
            te = [self.sb([128, 16, 256], F32, "te") for _ in range(2)]
            to = [self.sb([128, 16, 256], BF16, "to") for _ in range(2)]
            n = 0
            for t in range(6):
                for (edge, slot) in ((2, 18), (0, 24)):
                    a, b_, o = ti[n % 2], te[n % 2], to[n % 2]
                    n += 1
                    ph.dma("sp", a.t[:], self.na_bias[l, 1, t].rearrange("h p q -> p h q"), a.r, writes=[a.r])
                    ph.dma("sp", b_.t[:], self.na_bias[l, edge, t].rearrange("h p q -> p h q"), b_.r, writes=[b_.r])
                    ph.op("dve", lambda e, a=a: e.tensor_scalar(out=a.t[:], in0=a.t[:], scalar1=car.t[:, 0:1],
                                                               scalar2=None, op0=ALU.mult),
                          reads=[a.r, car.r], writes=[a.r])
                    ph.op("dve", lambda e, a=a, b_=b_, o=o: e.scalar_tensor_tensor(
                        out=o.t[:], in0=b_.t[:], scalar=car.t[:, 1:2], in1=a.t[:], op0=ALU.mult, op1=ALU.add),
                        reads=[a.r, b_.r, car.r], writes=[o.r])
                    ph.dma("pool", self.nb_b[slot + t].rearrange("h p q -> p h q"), o.t[:], o.r, reads=[o.r])
            ph.emit()

    def phase_norm(self, l, final=False):
        nc = self.nc
        src = self.x_in if (l == 0 and not final) else self.xs
        with nc.cleanup_on_exit():
            ph = self.begin("nm")
            if not final:
                idf, idb = self.consts(ph)
                gcol = self.sb([128, 32], F32, "gcol", const=True)
                ph.dma("sp", gcol.t[:], self.norm_gc[l], gcol.r, writes=[gcol.r])
                hts = [self.sb([128, 32, TT], BF16, "hts") for _ in range(2)]
                pst = [self.ps([128, 8, 128], BF16, "pst") for _ in range(4)]
            else:
                gbc = self.sb([128, D], F32, "gbc", const=True)
                ph.dma("sp", gbc.t[:], self.fin_g.partition_broadcast(128), gbc.r, writes=[gbc.r])
                yts = [self.sb([128, D], F32, "yt") for _ in range(2)]
            xts = [self.sb([128, D], F32, "xt") for _ in range(2)]
            junk = self.sb([128, D], BF16, "junk")
            xns = [self.sb([128, D], BF16, "xn") for _ in range(2)]
            sts = [self.sb([128, 4], F32, "st") for _ in range(2)]
            for it in range(NT // 128):
                xt, xn, st = xts[it % 2], xns[it % 2], sts[it % 2]
                ph.dma("sp", xt.t[:], src[it * 128:(it + 1) * 128, :], xt.r, writes=[xt.r])
                if l == 0 and not final:
                    ph.dma("pool", self.xs[it * 128:(it + 1) * 128, :], xt.t[:], xt.r, reads=[xt.r])
                ph.op("act", lambda e, xt=xt, st=st: e.activation(out=junk.t[:], in_=xt.t[:], func=AF.Square,
                                                                    accum_out=st.t[:, 0:1]),
                      reads=[xt.r], writes=[junk.r, st.r])
                ph.op("dve", lambda e, st=st: e.tensor_scalar(out=st.t[:, 1:2], in0=st.t[:, 0:1], scalar1=1.0 / D,
                                                               scalar2=EPS, op0=ALU.mult, op1=ALU.add),
                      reads=[st.r], writes=[st.r])
                ph.op("act", lambda e, st=st: e.activation(out=st.t[:, 2:3], in_=st.t[:, 1:2], func=AF.Sqrt),
                      reads=[st.r], writes=[st.r])
                ph.op("dve", lambda e, st=st: e.reciprocal(out=st.t[:, 3:4], in_=st.t[:, 2:3]),
                      reads=[st.r], writes=[st.r])
                if final:
                    yt = yts[it % 2]
                    ph.op("dve", lambda e, xt=xt, st=st, yt=yt: e.scalar_tensor_tensor(
                        out=yt.t[:], in0=xt.t[:], scalar=st.t[:, 3:4], in1=gbc.t[:], op0=ALU.mult, op1=ALU.mult),
                        reads=[xt.r, st.r, gbc.r], writes=[yt.r])
                    ph.dma("pool", self.y_out[it * 128:(it + 1) * 128, :], yt.t[:], yt.r, reads=[yt.r])
                    continue
                ph.op("dve", lambda e, xt=xt, st=st, xn=xn: e.tensor_scalar(
                    out=xn.t[:], in0=xt.t[:], scalar1=st.t[:, 3:4], scalar2=None, op0=ALU.mult),
                    reads=[xt.r, st.r], writes=[xn.r])
                ht = hts[(it // 4) % 2]
                sub = it % 4
                for k8 in range(4):
                    pt = pst[k8]

                    def tr(e, xn=xn, pt=pt, k8=k8):
                        for j in range(8):
                            kk = k8 * 8 + j
                            ins = e.transpose(pt.t[:, j, :], xn.t[:, kk * 128:(kk + 1) * 128], idb.t[:])
                        return ins
                    ph.op("pe", tr, reads=[xn.r, idb.r], writes=[pt.r])

                    def ev(e, pt=pt, ht=ht, k8=k8, sub=sub):
                        for j in range(8):
                            kk = k8 * 8 + j
                            ins = e.tensor_scalar(out=ht.t[:, kk, sub * 128:(sub + 1) * 128], in0=pt.t[:, j, :],
                                                  scalar1=gcol.t[:, kk:kk + 1], scalar2=None, op0=ALU.mult)
                        return ins
                    ph.op("dve", ev, reads=[pt.r, gcol.r], writes=[ht.r])
                if sub == 3:
                    tt = it // 4
                    ph.dma("pool", self.hT[:, :, tt * TT:(tt + 1) * TT].rearrange("k p t -> p k t"), ht.t[:],
                           ht.r, reads=[ht.r])
            ph.emit()

    def phase_inproj(self, l):
        nc = self.nc
        jobs = []

        def add(c0, n, mode, epi, dest, scale=1.0):
            for b in range(0, n, 512):
                w = min(512, n - b)
                d = dest + (b // 128 if mode == "FM" else b)
                jobs.append((c0 + b, w, [(mode, epi, scale, d)]))
        add(0, 2048, "FM", "copy", FM_AU)
        add(2048, 2048, "TM", "copy", TM_AV)
        add(4096, 2048, "FM", "silu", FM_ASZ)
        add(6144, 1024, "FM", "scale", FM_BQ, 256 ** -0.5)
        for b in range(0, 1024, 512):
            jobs.append((7168 + b, 512, [("FM", "copy", 1.0, FM_BK + b // 128), ("TM", "copy", 1.0, TM_BK + b)]))
        add(8192, 2048, "TM", "copy", TM_BV)
        add(10240, 2048, "TM", "silu", TM_BSZ)
        add(12288, 32, "FM", "copy", FM_LR)
        add(12320, 2048, "FM", "scale", FM_CQ, 128 ** -0.5)
        add(14368, 2048, "FM", "copy", FM_CK)
        add(16416, 2048, "TM", "copy", TM_CV)
        add(18464, 2048, "FM", "silu", FM_CSZ)
        with nc.cleanup_on_exit():
            ph = self.begin("ip")
            hts = [self.sb([128, 32, TT], BF16, "ht") for _ in range(2)]
            wts = [self.sb([128, 32, 512], BF16, "wt") for _ in range(3)]
            pss = [self.ps(name="ps") for _ in range(6)]
            stg = [self.sb([128, 512], BF16, "stg") for _ in range(6)]
            wv = self.w_in_b.rearrange("(k p) n -> p k n", p=128)
            nj = 0
            ne = 0
            for tt in range(NTT):
                ht = hts[tt % 2]
                ph.dma("sp", ht.t[:], self.hT[:, :, tt * TT:(tt + 1) * TT].rearrange("k p t -> p k t"), ht.r,
                       writes=[ht.r])
                for (c0, w, subs) in jobs:
                    wt = wts[nj % 3]
                    nj += 1
                    ph.dma("sp", wt.t[:, :, 0:w], wv[:, :, c0:c0 + w], wt.r, writes=[wt.r])
                    for (mode, epi, scale, dest) in subs:
                        nsub = (w + 127) // 128 if mode == "FM" else TT // 128
                        for s in range(nsub):
                            p = pss[ne % 6]
                            sg = stg[ne % 6]
                            ne += 1
                            if mode == "FM":
                                m = min(128, w - s * 128)

                                def mm(e, p=p, wt=wt, ht=ht, s=s, m=m):
                                    for k in range(32):
                                        ins = e.matmul(p.t[0:m, :], lhsT=wt.t[:, k, s * 128:s * 128 + m],
                                                       rhs=ht.t[:, k, :], start=(k == 0), stop=(k == 31))
                                    return ins
                                pv, sv = p.t[0:m, :], sg.t[0:m, :]
                                dst = self.fms[dest + s, 0:m, tt * TT:(tt + 1) * TT]
                            else:
                                def mm(e, p=p, wt=wt, ht=ht, s=s, w=w):
                                    for k in range(32):
                                        ins = e.matmul(p.t[:, 0:w], lhsT=ht.t[:, k, s * 128:(s + 1) * 128],
                                                       rhs=wt.t[:, k, 0:w], start=(k == 0), stop=(k == 31))
                                    return ins
                                pv, sv = p.t[:, 0:w], sg.t[:, 0:w]
                                dst = self.tms[tt * TT + s * 128:tt * TT + (s + 1) * 128, dest:dest + w]
                            ph.op("pe", mm, reads=[wt.r, ht.r], writes=[p.r])
                            if epi == "silu":
                                ph.op("act", lambda e, pv=pv, sv=sv: e.activation(out=sv, in_=pv, func=AF.Silu),
                                      reads=[p.r], writes=[sg.r])
                            elif epi == "scale":
                                ph.op("dve", lambda e, pv=pv, sv=sv, scale=scale: e.tensor_scalar(
                                    out=sv, in0=pv, scalar1=scale, scalar2=None, op0=ALU.mult),
                                    reads=[p.r], writes=[sg.r])
                            elif ne % 4 == 0:
                                ph.op("act", lambda e, pv=pv, sv=sv: e.activation(out=sv, in_=pv, func=AF.Copy),
                                      reads=[p.r], writes=[sg.r])
                            else:
                                ph.op("dve", lambda e, pv=pv, sv=sv: e.tensor_copy(out=sv, in_=pv),
                                      reads=[p.r], writes=[sg.r])
                            ph.dma("pool", dst, sv, sg.r, reads=[sg.r])
            ph.emit()

    def phase_sgu(self, l):
        nc = self.nc
        with nc.cleanup_on_exit():
            ph = self.begin("sg")
            lng = self.sb([128, BW], F32, "lng", const=True)
            ph.dma("sp", lng.t[:], self.sgu_lng[l].partition_broadcast(128), lng.r, writes=[lng.r])
            wsf = self.sb([128, 8, 128], F32, "wsf")
            ph.dma("sp", wsf.t[:], self.sgu_wT[l], wsf.r, writes=[wsf.r])
            wsb = self.sb([128, 8, 128], BF16, "wsb", const=True)
            ph.op("dve", lambda e: e.tensor_copy(out=wsb.t[:], in_=wsf.t[:]), reads=[wsf.r], writes=[wsb.r])
            bsb = self.sb([128, 8, 128], F32, "bsb", const=True)
            ph.dma("sp", bsb.t[:], self.sgu_b[l].partition_broadcast(128).rearrange("p (g i) -> p g i", g=8),
                   bsb.r, writes=[bsb.r])
            bs4 = self.sb([128, 8, 4, 128], F32, "bs4", const=True)

            def cpb(e):
                for c in range(4):
                    ins = e.tensor_copy(out=bs4.t[:, :, c, :], in_=bsb.t[:])
                return ins
            ph.op("dve", cpb, reads=[bsb.r], writes=[bs4.r])
            vts = [self.sb([128, 4, BW], BF16, "vt") for _ in range(2)]
            uts = [self.sb([128, 16, TT], BF16, "ut") for _ in range(2)]
            zts = [self.sb([128, 16, TT], BF16, "zt") for _ in range(2)]
            vn = self.sb([128, BW], F32, "vn")
            vnb = [self.sb([128, 4, BW], BF16, "vnb") for _ in range(1)]
            junk = self.sb([128, BW], BF16, "junk")
            sts = [self.sb([128, 8], F32, "st") for _ in range(2)]
            pss = [self.ps([128, 4, 128], F32, "ps") for _ in range(4)]
            t1s = [self.sb([128, 4, 128], F32, "t1") for _ in range(2)]
            yst = [self.sb([128, 16, TT], BF16, "ys") for _ in range(2)]
            nst = 0
            for tt in range(NTT):
                vt, ut, zt, vb, ys = vts[tt % 2], uts[tt % 2], zts[tt % 2], vnb[0], yst[tt % 2]
                t0 = tt * TT
                ph.dma("sp", vt.t[:], self.tms[t0:t0 + TT, TM_AV:TM_AV + BW].rearrange("(c p) n -> p c n", p=128),
                       vt.r, writes=[vt.r])
                ph.dma("sp", ut.t[:], self.fms[FM_AU:FM_AU + 16, :, t0:t0 + TT].rearrange("k p t -> p k t"),
                       ut.r, writes=[ut.r])
                ph.dma("sp", zt.t[:], self.fms[FM_ASZ:FM_ASZ + 16, :, t0:t0 + TT].rearrange("k p t -> p k t"),
                       zt.r, writes=[zt.r])
                for c in range(4):
                    st = sts[nst % 2]
                    nst += 1
                    vc = vt.t[:, c, :]
                    ph.op("dve", lambda e, vc=vc, st=st: e.reduce_sum(out=st.t[:, 0:1], in_=vc, axis=AX.X),
                          reads=[vt.r], writes=[st.r])
                    ph.op("act", lambda e, vc=vc, st=st: e.activation(out=junk.t[:], in_=vc, func=AF.Square,
                                                                       accum_out=st.t[:, 1:2]),
                          reads=[vt.r], writes=[junk.r, st.r])
                    ph.op("dve", lambda e, st=st: e.tensor_scalar(out=st.t[:, 2:3], in0=st.t[:, 0:1],
                                                                   scalar1=1.0 / BW, scalar2=None, op0=ALU.mult),
                          reads=[st.r], writes=[st.r])
                    ph.op("dve", lambda e, st=st: e.tensor_tensor(out=st.t[:, 3:4], in0=st.t[:, 2:3],
                                                                   in1=st.t[:, 2:3], op=ALU.mult),
                          reads=[st.r], writes=[st.r])
                    ph.op("dve", lambda e, st=st: e.scalar_tensor_tensor(
                        out=st.t[:, 4:5], in0=st.t[:, 1:2], scalar=1.0 / BW, in1=st.t[:, 3:4],
                        op0=ALU.mult, op1=ALU.subtract), reads=[st.r], writes=[st.r])
                    ph.op("dve", lambda e, st=st: e.tensor_scalar(out=st.t[:, 4:5], in0=st.t[:, 4:5], scalar1=EPS,
                                                                   scalar2=None, op0=ALU.add),
                          reads=[st.r], writes=[st.r])
                    ph.op("act", lambda e, st=st: e.activation(out=st.t[:, 5:6], in_=st.t[:, 4:5], func=AF.Sqrt),
                          reads=[st.r], writes=[st.r])
                    ph.op("dve", lambda e, st=st: e.reciprocal(out=st.t[:, 6:7], in_=st.t[:, 5:6]),
                          reads=[st.r], writes=[st.r])
                    ph.op("dve", lambda e, st=st: e.scalar_tensor_tensor(
                        out=st.t[:, 7:8], in0=st.t[:, 2:3], scalar=-1.0, in1=st.t[:, 6:7],
                        op0=ALU.mult, op1=ALU.mult), reads=[st.r], writes=[st.r])
                    ph.op("act", lambda e, vc=vc, st=st: e.activation(out=vn.t[:], in_=vc, func=AF.Identity,
                                                                       bias=st.t[:, 7:8], scale=st.t[:, 6:7]),
                          reads=[vt.r, st.r], writes=[vn.r])
                    ph.op("dve", lambda e, vb=vb, c=c: e.tensor_tensor(out=vb.t[:, c, :], in0=vn.t[:],
                                                                        in1=lng.t[:], op=ALU.mult),
                          reads=[vn.r, lng.r], writes=[vb.r])
                for cc in range(16):
                    g = cc // 2
                    p = pss[cc % 4]
                    t1 = t1s[cc % 2]

                    def mm(e, p=p, vb=vb, cc=cc, g=g):
                        for c in range(4):
                            ins = e.matmul(p.t[:, c, :], lhsT=vb.t[:, c, cc * 128:(cc + 1) * 128],
                                           rhs=wsb.t[:, g, :], start=True, stop=True)
                        return ins
                    ph.op("pe", mm, reads=[vb.r, wsb.r], writes=[p.r])
                    ph.op("dve", lambda e, p=p, t1=t1, g=g: e.tensor_tensor(
                        out=t1.t[:], in0=p.t[:], in1=bs4.t[:, g, :, :], op=ALU.add),
                        reads=[p.r, bs4.r], writes=[t1.r])
                    t1f = t1.t[:].rearrange("p c i -> p (c i)")
                    ph.op("dve", lambda e, t1f=t1f, ut=ut, cc=cc: e.tensor_tensor(
                        out=t1f, in0=t1f, in1=ut.t[:, cc, :], op=ALU.mult), reads=[t1.r, ut.r], writes=[t1.r])
                    ph.op("dve", lambda e, t1f=t1f, zt=zt, ys=ys, cc=cc: e.tensor_tensor(
                        out=ys.t[:, cc, :], in0=t1f, in1=zt.t[:, cc, :], op=ALU.mult),
                        reads=[t1.r, zt.r], writes=[ys.r])
                ph.dma("pool", self.yT[0:16, :, t0:t0 + TT].rearrange("k p t -> p k t"), ys.t[:], ys.r,
                       reads=[ys.r])
            ph.emit()

    def phase_gla(self, l, rev):
        nc = self.nc
        GT_ = 256
        ntile = NT // GT_
        d = 1 if rev else 0
        with nc.cleanup_on_exit():
            ph = self.begin("gb" if rev else "gf")
            triA = self.sb([64, 64], F32, "triA")
            triB = self.sb([64, 64], F32, "triB")
            mskA = self.sb([64, 4, 64], F32, "mskA")
            sA = 1 if not rev else -1
            ph.op("pool", lambda e: e.memset(triA.t[:], -1.0 / 16), writes=[triA.r])
            ph.op("pool", lambda e: e.affine_select(out=triA.t[:], in_=triA.t[:], pattern=[[sA, 64]],
                                                     compare_op=ALU.is_ge, fill=0.0, base=0, channel_multiplier=-sA),
                  reads=[triA.r], writes=[triA.r])
            ph.op("pool", lambda e: e.memset(triB.t[:], -1.0 / 16), writes=[triB.r])
            ph.op("pool", lambda e: e.affine_select(out=triB.t[:], in_=triB.t[:], pattern=[[-sA, 64]],
                                                     compare_op=ALU.is_gt, fill=0.0, base=0, channel_multiplier=sA),
                  reads=[triB.r], writes=[triB.r])
            ph.op("pool", lambda e: e.memset(mskA.t[:], 1.0), writes=[mskA.r])
            ph.op("pool", lambda e: e.affine_select(out=mskA.t[:], in_=mskA.t[:], pattern=[[0, 4], [sA, 64]],
                                                     compare_op=ALU.is_ge, fill=0.0, base=0, channel_multiplier=-sA),
                  reads=[mskA.r], writes=[mskA.r])
            triA.r.const = triB.r.const = mskA.r.const = True
            wgf = self.sb([17, 1024], F32, "wgf")
            ph.dma("sp", wgf.t[:], self.gla_wgk[l, d], wgf.r, writes=[wgf.r])
            wgb = self.sb([17, 1024], BF16, "wgb", const=True)
            ph.op("dve", lambda e: e.tensor_copy(out=wgb.t[:], in_=wgf.t[:]), reads=[wgf.r], writes=[wgb.r])
            lrt = self.sb([17, NT], BF16, "lrt", const=True)
            ph.op("pool", lambda e: e.memset(lrt.t[:], 1.0), writes=[lrt.r])
            ph.dma("sp", lrt.t[0:16, :], self.fms[FM_LR, d * 16:(d + 1) * 16, :], lrt.r, reads=[lrt.r],
                   writes=[lrt.r])
            car = self.sb([128, 1], F32, "car", const=True)
            ph.dma("sp", car.t[:], self.carry, car.r, writes=[car.r])
            S = self.sb([128, 8, 512], F32, "S")
            Sb = self.sb([128, 8, 512], BF16, "Sb")
            ph.op("pool", lambda e: e.memset(S.t[:], 0.0), writes=[S.r])
            ph.op("pool", lambda e: e.memset(Sb.t[:], 0.0), writes=[Sb.r])
            qts = [self.sb([128, 8, GT_], BF16, "qt") for _ in range(2)]
            kts = [self.sb([128, 8, GT_], BF16, "kt") for _ in range(2)]
            kms = [self.sb([64, 4, 1024], BF16, "km") for _ in range(2)]
            vms = [self.sb([64, 4, BW], BF16, "vm") for _ in range(2)]
            ex = self.sb([64, 1024], F32, "ex")
            Lg = self.sb([64, 1024], F32, "Lg")
            Ep = self.sb([128, 8, 64], F32, "Ep")
            Em = self.sb([128, 8, 64], F32, "Em")
            Er = self.sb([64, 1024], F32, "Er")
            qe = self.sb([128, 8, 64], BF16, "qe")
            ke = self.sb([128, 8, 64], BF16, "ke")
            kk = self.sb([64, 1024], BF16, "kk")
            aTm = self.sb([64, 4, 64], BF16, "aTm")
            ost = [self.sb([64, BW], F32, "ost") for _ in range(2)]
            pX = [self.ps(name="pX") for _ in range(2)]
            pG = self.ps([128, 8, 64], F32, "pG")
            pA = self.ps([64, 4, 64], F32, "pA")
            pO = [self.ps(name="pO") for _ in range(2)]
            pU = [self.ps(name="pU") for _ in range(2)]
            fms, tms = self.fms, self.tms
            order = list(range(ntile))
            if rev:
                order = order[::-1]
            nch = 0
            for ti, tile in enumerate(order):
                t0 = tile * GT_
                qt, kt, km, vm = qts[ti % 2], kts[ti % 2], kms[ti % 2], vms[ti % 2]
                ph.dma("sp", qt.t[:], fms[FM_BQ:FM_BQ + 8, :, t0:t0 + GT_].rearrange("k p t -> p k t"), qt.r,
                       writes=[qt.r])
                ph.dma("sp", kt.t[:], fms[FM_BK:FM_BK + 8, :, t0:t0 + GT_].rearrange("k p t -> p k t"), kt.r,
                       writes=[kt.r])
                ph.dma("sp", km.t[:], tms[t0:t0 + GT_, TM_BK:TM_BK + 1024].rearrange("(c p) n -> p c n", p=64),
                       km.r, writes=[km.r])
                ph.dma("sp", vm.t[:], tms[t0:t0 + GT_, TM_BV:TM_BV + BW].rearrange("(c p) n -> p c n", p=64),
                       vm.r, writes=[vm.r])
                cs = [3, 2, 1, 0] if rev else [0, 1, 2, 3]
                for c in cs:
                    tk0 = t0 + c * 64
                    if (not rev and tk0 == NT // 2) or (rev and tk0 == NT // 2 - 64):
                        ph.op("dve", lambda e: e.tensor_scalar(out=S.t[:], in0=S.t[:], scalar1=car.t[:, 0:1],
                                                               scalar2=None, op0=ALU.mult),
                              reads=[S.r, car.r], writes=[S.r])
                        ph.op("pool", lambda e: e.tensor_copy(out=Sb.t[:], in_=S.t[:]), reads=[S.r], writes=[Sb.r])

                    def mm1(e, tk0=tk0):
                        for hf in range(2):
                            ins = e.matmul(pX[hf].t[0:64, :], lhsT=lrt.t[0:17, tk0:tk0 + 64],
                                           rhs=wgb.t[0:17, hf * 512:(hf + 1) * 512], start=True, stop=True)
                        return ins
                    ph.op("pe", mm1, reads=[lrt.r, wgb.r], writes=[pX[0].r, pX[1].r])

                    def a1(e):
                        for hf in range(2):
                            ins = e.activation(out=ex.t[:, hf * 512:(hf + 1) * 512], in_=pX[hf].t[0:64, :],
                                               func=AF.Exp, scale=-1.0)
                        return ins
                    ph.op("act", a1, reads=[pX[0].r, pX[1].r], writes=[ex.r])
                    ph.op("act", lambda e: e.activation(out=Lg.t[:], in_=ex.t[:], func=AF.Ln, bias=1.0),
                          reads=[ex.r], writes=[Lg.r])

                    def mm2(e):
                        for dc in range(8):
                            ins = e.matmul(pG.t[:, dc, :], lhsT=Lg.t[:, dc * 128:(dc + 1) * 128], rhs=triA.t[:],
                                           start=True, stop=True)
                        for hf in range(2):
                            ins = e.matmul(pX[hf].t[0:64, :], lhsT=triB.t[:], rhs=Lg.t[:, hf * 512:(hf + 1) * 512],
                                           start=True, stop=True)
                        return ins
                    ph.op("pe", mm2, reads=[Lg.r, triA.r, triB.r, ex.r], writes=[pG.r, pX[0].r, pX[1].r])
                    ph.op("act", lambda e: e.activation(out=Ep.t[:], in_=pG.t[:], func=AF.Exp),
                          reads=[pG.r], writes=[Ep.r])
                    ph.op("act", lambda e: e.activation(out=Em.t[:], in_=pG.t[:], func=AF.Exp, scale=-1.0),
                          reads=[pG.r], writes=[Em.r])

                    def a3(e):
                        for hf in range(2):
                            ins = e.activation(out=Er.t[:, hf * 512:(hf + 1) * 512], in_=pX[hf].t[0:64, :],
                                               func=AF.Exp)
                        return ins
                    ph.op("act", a3, reads=[pX[0].r, pX[1].r], writes=[Er.r])
                    ph.op("dve", lambda e, qt=qt, c=c: e.tensor_tensor(
                        out=qe.t[:], in0=qt.t[:, :, c * 64:(c + 1) * 64], in1=Ep.t[:], op=ALU.mult),
                        reads=[qt.r, Ep.r], writes=[qe.r])
                    ph.op("dve", lambda e, kt=kt, c=c: e.tensor_tensor(
                        out=ke.t[:], in0=kt.t[:, :, c * 64:(c + 1) * 64], in1=Em.t[:], op=ALU.mult),
                        reads=[kt.r, Em.r], writes=[ke.r])
                    ph.op("dve", lambda e, km=km, c=c: e.tensor_tensor(
                        out=kk.t[:], in0=km.t[:, c, :], in1=Er.t[:], op=ALU.mult),
                        reads=[km.r, Er.r], writes=[kk.r])

                    def mm3(e):
                        for h in range(4):
                            for q in range(2):
                                dc = 2 * h + q
                                ins = e.matmul(pA.t[:, h, :], lhsT=ke.t[:, dc, :], rhs=qe.t[:, dc, :],
                                               start=(q == 0), stop=(q == 1))
                        return ins
                    ph.op("pe", mm3, reads=[ke.r, qe.r], writes=[pA.r])
                    ph.op("dve", lambda e: e.tensor_tensor(out=aTm.t[:], in0=pA.t[:], in1=mskA.t[:], op=ALU.mult),
                          reads=[pA.r, mskA.r], writes=[aTm.r])
                    o_s = ost[nch % 2]
                    nch += 1
                    for h in range(4):
                        po = pO[h % 2]

                        def mm4(e, h=h, po=po, vm=vm, c=c):
                            e.matmul(po.t[0:64, :], lhsT=aTm.t[:, h, :], rhs=vm.t[:, c, h * 512:(h + 1) * 512],
                                     start=True, stop=False)
                            e.matmul(po.t[0:64, :], lhsT=qe.t[:, 2 * h, :], rhs=Sb.t[:, 2 * h, :],
                                     start=False, stop=False)
                            return e.matmul(po.t[0:64, :], lhsT=qe.t[:, 2 * h + 1, :], rhs=Sb.t[:, 2 * h + 1, :],
                                            start=False, stop=True)
                        ph.op("pe", mm4, reads=[aTm.r, vm.r, qe.r, Sb.r], writes=[po.r])
                        ph.op("act", lambda e, h=h, po=po, o_s=o_s: e.activation(
                            out=o_s.t[:, h * 512:(h + 1) * 512], in_=po.t[0:64, :], func=AF.Copy),
                            reads=[po.r], writes=[o_s.r])
                    ph.dma("pool", self.og[d, tk0:tk0 + 64, :], o_s.t[:], o_s.r, reads=[o_s.r])
                    gl = 0 if rev else 63
                    for dc in range(8):
                        pu = pU[dc % 2]
                        h = dc // 2

                        def mm5(e, dc=dc, pu=pu, h=h, vm=vm, c=c):
                            return e.matmul(pu.t[:], lhsT=kk.t[:, dc * 128:(dc + 1) * 128],
                                            rhs=vm.t[:, c, h * 512:(h + 1) * 512], start=True, stop=True)
                        ph.op("pe", mm5, reads=[kk.r, vm.r], writes=[pu.r])
                        ph.op("dve", lambda e, dc=dc, pu=pu, gl=gl: e.scalar_tensor_tensor(
                            out=S.t[:, dc, :], in0=S.t[:, dc, :], scalar=Ep.t[:, dc, gl:gl + 1], in1=pu.t[:],
                            op0=ALU.mult, op1=ALU.add), reads=[S.r, Ep.r, pu.r], writes=[S.r])
                    ph.op("pool", lambda e: e.tensor_copy(out=Sb.t[:], in_=S.t[:]), reads=[S.r], writes=[Sb.r])
            ph.emit()

    def phase_gla_out(self, l):
        nc = self.nc
        with nc.cleanup_on_exit():
            ph = self.begin("go")
            idf, idb = self.consts(ph)
            ngb = self.sb([128, 512], F32, "ngb", const=True)
            ph.dma("sp", ngb.t[:], self.gla_ng[l].partition_broadcast(128), ngb.r, writes=[ngb.r])
            ofs = [self.sb([128, BW], F32, "of") for _ in range(2)]
            obs = [self.sb([128, BW], F32, "ob") for _ in range(2)]
            szs = [self.sb([128, BW], BF16, "sz") for _ in range(2)]
            junk = self.sb([128, 512], BF16, "junk")
            sts = [self.sb([128, 16], F32, "st") for _ in range(2)]
            yb = [self.sb([128, BW], BF16, "yb") for _ in range(2)]
            pst = [self.ps([128, 8, 128], BF16, "pst") for _ in range(2)]
            yts = [self.sb([128, 16, TT], BF16, "yt") for _ in range(2)]
            for it in range(NT // 128):
                of, ob, sz, st, y = ofs[it % 2], obs[it % 2], szs[it % 2], sts[it % 2], yb[it % 2]
                r0 = it * 128
                ph.dma("sp", of.t[:], self.og[0, r0:r0 + 128, :], of.r, writes=[of.r])
                ph.dma("sp", ob.t[:], self.og[1, r0:r0 + 128, :], ob.r, writes=[ob.r])
                ph.dma("sp", sz.t[:], self.tms[r0:r0 + 128, TM_BSZ:TM_BSZ + BW], sz.r, writes=[sz.r])
                ph.op("dve", lambda e, of=of, ob=ob: e.tensor_tensor(out=of.t[:], in0=of.t[:], in1=ob.t[:],
                                                                     op=ALU.add),
                      reads=[of.r, ob.r], writes=[of.r])

                def sq(e, of=of, st=st):
                    for h in range(4):
                        ins = e.activation(out=junk.t[:], in_=of.t[:, h * 512:(h + 1) * 512], func=AF.Square,
                                           accum_out=st.t[:, h:h + 1])
                    return ins
                ph.op("act", sq, reads=[of.r], writes=[junk.r, st.r])
                ph.op("dve", lambda e, st=st: e.tensor_scalar(out=st.t[:, 4:8], in0=st.t[:, 0:4], scalar1=1.0 / 512,
                                                               scalar2=EPS, op0=ALU.mult, op1=ALU.add),
                      reads=[st.r], writes=[st.r])
                ph.op("act", lambda e, st=st: e.activation(out=st.t[:, 8:12], in_=st.t[:, 4:8], func=AF.Sqrt),
                      reads=[st.r], writes=[st.r])
                ph.op("dve", lambda e, st=st: e.reciprocal(out=st.t[:, 12:16], in_=st.t[:, 8:12]),
                      reads=[st.r], writes=[st.r])

                def nrm(e, of=of, st=st):
                    for h in range(4):
                        ins = e.scalar_tensor_tensor(out=of.t[:, h * 512:(h + 1) * 512],
                                                     in0=of.t[:, h * 512:(h + 1) * 512],
                                                     scalar=st.t[:, 12 + h:13 + h], in1=ngb.t[:],
                                                     op0=ALU.mult, op1=ALU.mult)
                    return ins
                ph.op("dve", nrm, reads=[of.r, st.r, ngb.r], writes=[of.r])
                ph.op("dve", lambda e, of=of, sz=sz, y=y: e.tensor_tensor(out=y.t[:], in0=of.t[:], in1=sz.t[:],
                                                                           op=ALU.mult),
                      reads=[of.r, sz.r], writes=[y.r])
                yt = yts[(it // 4) % 2]
                sub = it % 4
                for k8 in range(2):
                    pt = pst[k8]

                    def tr(e, y=y, pt=pt, k8=k8):
                        for j in range(8):
                            kk_ = k8 * 8 + j
                            ins = e.transpose(pt.t[:, j, :], y.t[:, kk_ * 128:(kk_ + 1) * 128], idb.t[:])
                        return ins
                    ph.op("pe", tr, reads=[y.r, idb.r], writes=[pt.r])
                    ph.op("act", lambda e, pt=pt, yt=yt, k8=k8, sub=sub: e.activation(
                        out=yt.t[:, k8 * 8:(k8 + 1) * 8, sub * 128:(sub + 1) * 128], in_=pt.t[:], func=AF.Copy),
                        reads=[pt.r], writes=[yt.r])
                if sub == 3:
                    t0 = (it // 4) * TT
                    ph.dma("pool", self.yT[16:32, :, t0:t0 + TT].rearrange("k p t -> p k t"), yt.t[:], yt.r,
                           reads=[yt.r])
            ph.emit()

    def phase_na(self, l):
        nc = self.nc
        NB = NT // 256
        with nc.cleanup_on_exit():
            ph = self.begin("na")
            idf, idb = self.consts(ph)
            ones = self.sb([128, 128], BF16, "ones", const=True)
            ph.op("pool", lambda e: e.memset(ones.t[:], 1.0), writes=[ones.r])
            qts = [self.sb([128, NT], BF16, "q") for _ in range(2)]
            kts = [self.sb([128, NT], BF16, "k") for _ in range(2)]
            vts = [self.sb([128, 32, 128], BF16, "v") for _ in range(2)]
            zts = [self.sb([128, NT], BF16, "z") for _ in range(2)]
            bts = [self.sb([128, 30, 256], BF16, "b") for _ in range(2)]
            yts = [self.sb([128, NT], BF16, "y") for _ in range(2)]
            pS = [self.ps([128, 256], F32, "pS") for _ in range(3)]
            pO = [self.ps([128, 256], F32, "pO") for _ in range(2)]
            pD = [self.ps([128, 256], F32, "pD") for _ in range(2)]
            pts = [self.sb([128, 256], BF16, "pt") for _ in range(3)]
            rd = [self.sb([128, 256], F32, "rd") for _ in range(2)]
            o1 = [self.sb([128, 256], F32, "o1") for _ in range(2)]
            nS = 0
            nB = 0
            for h in range(16):
                qt, kt, vt, zt, bt, yt = qts[h % 2], kts[h % 2], vts[h % 2], zts[h % 2], bts[h % 2], yts[h % 2]
                ph.dma("sp", qt.t[:], self.fms[FM_CQ + h], qt.r, writes=[qt.r])
                ph.dma("sp", kt.t[:], self.fms[FM_CK + h], kt.r, writes=[kt.r])
                ph.dma("sp", zt.t[:], self.fms[FM_CSZ + h], zt.r, writes=[zt.r])
                ph.dma("sp", vt.t[:], self.tms[:, TM_CV + h * 128:TM_CV + (h + 1) * 128].rearrange(
                    "(c p) n -> p c n", p=128), vt.r, writes=[vt.r])
                ph.dma("sp", bt.t[:], self.nb_b[:, h].rearrange("a p q -> p a q"), bt.r, writes=[bt.r])
                for b in range(NB):
                    kind = {0: 0, NB - 1: 2, NB // 2 - 1: 3, NB // 2: 4}.get(b, 1)
                    tl = [t for t in range(6) if 0 <= 4 * b - 4 + 2 * t and 4 * b - 4 + 2 * t + 2 <= NT // 64]
                    po, pd = pO[nB % 2], pD[nB % 2]
                    q0 = b * 256
                    for ix, t in enumerate(tl):
                        kc = (4 * b - 4 + 2 * t) // 2
                        p_s = pS[nS % 3]
                        pt = pts[nS % 3]
                        nS += 1

                        def mms(e, p_s=p_s, kt=kt, qt=qt, bt=bt, kc=kc, q0=q0, kind=kind, t=t):
                            e.matmul(p_s.t[:], lhsT=kt.t[:, kc * 128:(kc + 1) * 128], rhs=qt.t[:, q0:q0 + 256],
                                     start=True, stop=False)
                            return e.matmul(p_s.t[:], lhsT=idb.t[:], rhs=bt.t[:, kind * 6 + t, :],
                                            start=False, stop=True)
                        ph.op("pe", mms, reads=[kt.r, qt.r, bt.r, idb.r], writes=[p_s.r])
                        ph.op("act", lambda e, p_s=p_s, pt=pt: e.activation(out=pt.t[:], in_=p_s.t[:], func=AF.Exp),
                              reads=[p_s.r], writes=[pt.r])

                        def mmo(e, po=po, pd=pd, vt=vt, pt=pt, kc=kc, first=(ix == 0), last=(ix == len(tl) - 1)):
                            e.matmul(po.t[:], lhsT=vt.t[:, kc, :], rhs=pt.t[:], start=first, stop=last)
                            return e.matmul(pd.t[:], lhsT=ones.t[:], rhs=pt.t[:], start=first, stop=last)
                        ph.op("pe", mmo, reads=[vt.r, pt.r, ones.r], writes=[po.r, pd.r])
                    r_, o_ = rd[nB % 2], o1[nB % 2]
                    nB += 1
                    ph.op("dve", lambda e, r_=r_, pd=pd: e.reciprocal(out=r_.t[:], in_=pd.t[:]),
                          reads=[pd.r], writes=[r_.r])
                    ph.op("dve", lambda e, o_=o_, po=po, r_=r_: e.tensor_tensor(out=o_.t[:], in0=po.t[:],
                                                                                   in1=r_.t[:], op=ALU.mult),
                          reads=[po.r, r_.r], writes=[o_.r])
                    ph.op("dve", lambda e, o_=o_, zt=zt, yt=yt, q0=q0: e.tensor_tensor(
                        out=yt.t[:, q0:q0 + 256], in0=o_.t[:], in1=zt.t[:, q0:q0 + 256], op=ALU.mult),
                        reads=[o_.r, zt.r], writes=[yt.r])
                ph.dma("pool", self.yT[32 + h], yt.t[:], yt.r, reads=[yt.r])
            ph.emit()

    def phase_merge(self, l):
        nc = self.nc
        with nc.cleanup_on_exit():
            ph = self.begin("mg")
            bg = self.sb([128, 96], F32, "bg", const=True)
            ph.dma("sp", bg.t[:], self.b_gatec[l], bg.r, writes=[bg.r])
            ht = self.sb([128, 32, TT], BF16, "ht")
            yt = self.sb([128, 48, TT], BF16, "yt")
            wgs = [self.sb([128, 32, 128], BF16, "wg") for _ in range(3)]
            wbs = [self.sb([128, 16, 128], BF16, "wb") for _ in range(3)]
            pG = [self.ps(name="pG") for _ in range(4)]
            pB = [self.ps(name="pB") for _ in range(4)]
            gs = [self.sb([128, TT], F32, "g") for _ in range(4)]
            ms = [self.sb([128, TT], F32, "m") for _ in range(2)]
            mts = [self.sb([128, 32, TT], BF16, "mt") for _ in range(1)]
            n3 = 0
            nf = 0
            for tt in range(NTT):
                t0 = tt * TT
                mt = mts[0]
                ph.dma("sp", ht.t[:], self.hT[:, :, t0:t0 + TT].rearrange("k p t -> p k t"), ht.r, writes=[ht.r])
                ph.dma("sp", yt.t[:], self.yT[:, :, t0:t0 + TT].rearrange("k p t -> p k t"), yt.r, writes=[yt.r])
                for f in range(32):
                    m = ms[nf % 2]
                    nf += 1
                    for b in range(3):
                        pg, pb, g = pG[n3 % 4], pB[n3 % 4], gs[n3 % 4]
                        wg, wb = wgs[n3 % 3], wbs[n3 % 3]
                        n3 += 1
                        ph.dma("sp", wg.t[:], self.wg_s[f, b].rearrange("p (k c) -> p k c", k=32), wg.r,
                               writes=[wg.r])
                        ph.dma("sp", wb.t[:], self.wb_s[f, b].rearrange("p (k c) -> p k c", k=16), wb.r,
                               writes=[wb.r])

                        def mmg(e, pg=pg, wg=wg, b=b):
                            for k in range(32):
                                ins = e.matmul(pg.t[:], lhsT=wg.t[:, k, :], rhs=ht.t[:, k, :],
                                               start=(k == 0), stop=(k == 31))
                            return ins
                        ph.op("pe", mmg, reads=[wg.r, ht.r], writes=[pg.r])

                        def mmb(e, pb=pb, wb=wb, b=b):
                            for k in range(16):
                                ins = e.matmul(pb.t[:], lhsT=wb.t[:, k, :], rhs=yt.t[:, b * 16 + k, :],
                                               start=(k == 0), stop=(k == 15))
                            return ins
                        ph.op("pe", mmb, reads=[wb.r, yt.r], writes=[pb.r])
                        col = b * 32 + f
                        ph.op("act", lambda e, pg=pg, g=g, col=col: e.activation(
                            out=g.t[:], in_=pg.t[:], func=AF.Sigmoid, bias=bg.t[:, col:col + 1]),
                            reads=[pg.r, bg.r], writes=[g.r])
                        if b == 0:
                            ph.op("dve", lambda e, m=m, g=g, pb=pb: e.tensor_tensor(out=m.t[:], in0=g.t[:],
                                                                                     in1=pb.t[:], op=ALU.mult),
                                  reads=[g.r, pb.r], writes=[m.r])
                        else:
                            ph.op("dve", lambda e, g=g, pb=pb: e.tensor_tensor(out=g.t[:], in0=g.t[:], in1=pb.t[:],
                                                                                op=ALU.mult),
                                  reads=[g.r, pb.r], writes=[g.r])
                            if b == 1:
                                ph.op("pool", lambda e, m=m, g=g: e.tensor_tensor(out=m.t[:], in0=m.t[:],
                                                                                   in1=g.t[:], op=ALU.add),
                                      reads=[m.r, g.r], writes=[m.r])
                            else:
                                ph.op("pool", lambda e, m=m, g=g, mt=mt, f=f: e.tensor_tensor(
                                    out=mt.t[:, f, :], in0=m.t[:], in1=g.t[:], op=ALU.add),
                                    reads=[m.r, g.r], writes=[mt.r])
                ph.dma("pool", self.mT[:, :, t0:t0 + TT].rearrange("k p t -> p k t"), mt.t[:], mt.r, reads=[mt.r])
            ph.emit()

    def phase_out(self, l):
        nc = self.nc
        with nc.cleanup_on_exit():
            ph = self.begin("op")
            mts = [self.sb([128, 32, TT], BF16, "mt") for _ in range(2)]
            wos = [self.sb([128, 32, 512], BF16, "wo") for _ in range(2)]
            pss = [self.ps(name="ps") for _ in range(4)]
            xss = [self.sb([128, 512], F32, "xs") for _ in range(4)]
            wv = self.w_out_b.rearrange("(k p) n -> p k n", p=128)
            nw = 0
            ne = 0
            for tt in range(NTT):
                t0 = tt * TT
                mt = mts[tt % 2]
                ph.dma("sp", mt.t[:], self.mT[:, :, t0:t0 + TT].rearrange("k p t -> p k t"), mt.r, writes=[mt.r])
                for n in range(8):
                    wo = wos[nw % 2]
                    nw += 1
                    ph.dma("sp", wo.t[:], wv[:, :, n * 512:(n + 1) * 512], wo.r, writes=[wo.r])
                    for s in range(4):
                        p, xs_ = pss[ne % 4], xss[ne % 4]
                        ne += 1
                        rows = slice(t0 + s * 128, t0 + (s + 1) * 128)
                        cols = slice(n * 512, (n + 1) * 512)
                        ph.dma("sp", xs_.t[:], self.xs[rows, cols], xs_.r, writes=[xs_.r])

                        def mm(e, p=p, mt=mt, wo=wo, s=s):
                            for k in range(32):
                                ins = e.matmul(p.t[:], lhsT=mt.t[:, k, s * 128:(s + 1) * 128], rhs=wo.t[:, k, :],
                                               start=(k == 0), stop=(k == 31))
                            return ins
                        ph.op("pe", mm, reads=[mt.r, wo.r], writes=[p.r])
                        ph.op("dve", lambda e, p=p, xs_=xs_: e.tensor_tensor(out=xs_.t[:], in0=xs_.t[:], in1=p.t[:],
                                                                              op=ALU.add),
                              reads=[p.r, xs_.r], writes=[xs_.r])
                        ph.dma("pool", self.xs[rows, cols], xs_.t[:], xs_.r, reads=[xs_.r])
            ph.emit()

    def build(self, upto=None):
        for l in range(self.depth):
            self.phase_convert(l)
            self.phase_norm(l)
            self.phase_inproj(l)
            self.phase_sgu(l)
            self.phase_gla(l, False)
            self.phase_gla(l, True)
            self.phase_gla_out(l)
            self.phase_na(l)
            self.phase_merge(l)
            self.phase_out(l)
        self.phase_norm(self.depth, final=True)
        return self.nc


def _na_bias_tiles(rpb_l):
    H = rpb_l.shape[0]
    cols = np.arange(64)
    cs = np.clip(cols - 8, 0, 48)
    col_ok = (cols[None, :] >= cs[:, None]) & (cols[None, :] < cs[:, None] + 16)
    dc = np.clip(cols[None, :] - cols[:, None] + 15, 0, 30)
    out = np.full((3, 6, H, 128, 256), NEG, np.float32)
    for kind in range(3):
        for t in range(6):
            for kr2 in range(2):
                ko = -4 + 2 * t + kr2
                for qr in range(4):
                    krel = ko - qr
                    if kind == 1:
                        ok = -4 <= krel <= 3
                    elif kind == 0:
                        ok = 0 <= ko <= 7
                    else:
                        ok = -4 <= ko <= 3
                    if not ok:
                        continue
                    blk = rpb_l[:, krel + 7, :][:, dc]
                    blk = np.where(col_ok[None], blk, np.float32(NEG))
                    out[kind, t, :, kr2 * 64:(kr2 + 1) * 64, qr * 64:(qr + 1) * 64] = blk.transpose(0, 2, 1)
    return out


_NC_CACHE = {}


def _layout(inputs, depth=DEPTH):
    f32 = np.float32
    xp = np.asarray(inputs["x_prompt"], f32)
    xsmp = np.asarray(inputs["x_sample"], f32)
    slabs = [xsmp[0], xsmp[1],
             np.concatenate([xp[0], xp[1]], 0), np.concatenate([xp[2], xp[3]], 0)]
    zero = np.zeros_like(xp[0])
    for i in range(4, 8):
        slabs.append(np.concatenate([xp[i], zero], 0))
    two_seq = [False, False] + [True] * 6
    L = depth
    shared = {
        "w_in": np.ascontiguousarray(np.asarray(inputs["w_in"], f32)[:L]),
        "w_gate": np.ascontiguousarray(np.asarray(inputs["w_gate"], f32)[:L]),
        "w_branch": np.ascontiguousarray(np.asarray(inputs["w_branch"], f32)[:L]),
        "w_out": np.ascontiguousarray(np.asarray(inputs["w_out"], f32)[:L]),
        "norm_gc": np.ascontiguousarray(np.asarray(inputs["norm_g"], f32)[:L].reshape(L, 32, 128).transpose(0, 2, 1)),
        "fin_g": np.asarray(inputs["final_norm_g"], f32),
        "b_gatec": np.ascontiguousarray(np.asarray(inputs["b_gate"], f32)[:L].reshape(L, 96, 128).transpose(0, 2, 1)),
        "sgu_lng": np.ascontiguousarray(np.asarray(inputs["sgu_ln_g"], f32)[:L]),
        "sgu_wT": np.ascontiguousarray(np.asarray(inputs["sgu_w"], f32)[:L].transpose(0, 3, 1, 2)),
        "sgu_b": np.ascontiguousarray(np.asarray(inputs["sgu_b"], f32)[:L].reshape(L, 1024)),
        "gla_wgk": np.ascontiguousarray(np.concatenate(
            [np.asarray(inputs["gla_w_gk"], f32)[:L], np.asarray(inputs["gla_b_gk"], f32)[:L, :, None, :]], axis=2)),
        "gla_ng": np.ascontiguousarray(np.asarray(inputs["gla_norm_g"], f32)[:L]),
    }
    rpb = np.asarray(inputs["na_rpb"], f32)[:L]
    shared["na_bias"] = np.stack([_na_bias_tiles(rpb[l]) for l in range(L)])
    in_maps = []
    for c in range(8):
        m = dict(shared)
        m["x"] = np.ascontiguousarray(slabs[c])
        m["carry"] = np.full((128, 1), 0.0 if two_seq[c] else 1.0, f32)
        in_maps.append(m)
    return in_maps


def kernel(**inputs):
    if "nc" not in _NC_CACHE:
        _NC_CACHE["nc"] = K().build()
    nc = _NC_CACHE["nc"]
    in_maps = _layout(inputs)
    res = run_bass_kernel_spmd(nc, in_maps, core_ids=list(range(8)))
    ys = [np.asarray(r["y"]) for r in res.results]
    y_sample = np.stack([ys[0], ys[1]], 0)
    y_prompt = np.stack([ys[2][:2048], ys[2][2048:], ys[3][:2048], ys[3][2048:],
                         ys[4][:2048], ys[5][:2048], ys[6][:2048], ys[7][:2048]], 0)
    return (y_prompt.astype(np.float32), y_sample.astype(np.float32))
```

```python
import numpy as np
import concourse.bass as bass
import concourse.mybir as mybir
from concourse.bass_utils import run_bass_kernel_spmd

F32 = mybir.dt.float32
BF16 = mybir.dt.bfloat16
AF = mybir.ActivationFunctionType
ALU = mybir.AluOpType
AX = mybir.AxisListType
ENG = ("pe", "act", "dve", "pool", "sp")

D = 4096
DEPTH = 4
NT = 4096
TT = 512
NTT = NT // TT
INC = 20512
BW = 2048
EPS = 1e-6
NEG = -30000.0

FM_AU, FM_ASZ, FM_BQ, FM_BK, FM_LR, FM_CQ, FM_CK, FM_CSZ, NFM = 0, 16, 32, 40, 48, 49, 65, 81, 97
TM_AV, TM_BK, TM_BV, TM_BSZ, TM_CV, NTM = 0, 2048, 3072, 5120, 7168, 9216


class Res:
    __slots__ = ("name", "lw", "rd", "sem", "cnt", "const")

    def __init__(self, name, const=False):
        self.name = name
        self.lw = None
        self.rd = {}
        self.sem = None
        self.cnt = 0
        self.const = const


class Op:
    __slots__ = ("eng", "fn", "deps", "sig", "val", "dres", "dval")

    def __init__(self, eng, fn):
        self.eng = eng
        self.fn = fn
        self.deps = []
        self.sig = False
        self.val = 0
        self.dres = None
        self.dval = 0


class Phase:
    def __init__(self, nc, name):
        self.nc = nc
        self.name = name
        self.ops = {e: [] for e in ENG}
        self.dres = []

    def _track(self, o, reads, writes):
        deps = o.deps
        for r in reads:
            if r.lw is not None:
                deps.append(r.lw)
        for r in writes:
            if r.lw is not None:
                deps.append(r.lw)
            deps.extend(r.rd.values())
        key = o.eng if o.dres is None else ("d", id(o.dres))
        for r in reads:
            if not r.const:
                r.rd[key] = o
        for r in writes:
            r.lw = o
            r.rd = {}
        for d in deps:
            d.sig = True
        self.ops[o.eng].append(o)
        return o

    def op(self, eng, fn, reads=(), writes=()):
        return self._track(Op(eng, fn), reads, writes)

    def dma(self, q, out, in_, key, reads=(), writes=()):
        o = Op(q, (out, in_))
        if key.sem is None:
            self.dres.append(key)
            key.sem = True
        key.cnt += 16
        o.dres = key
        o.dval = key.cnt
        return self._track(o, reads, writes)

    def emit(self):
        nc = self.nc
        esem = {e: nc.alloc_semaphore(f"{self.name}_{e}") for e in ENG}
        for i, r in enumerate(self.dres):
            r.sem = nc.alloc_semaphore(f"{self.name}_d{i}")
        for e in ENG:
            c = 0
            for o in self.ops[e]:
                if o.dres is None and o.sig:
                    c += 1
                    o.val = c
        lastq = {}
        for e in ENG:
            for o in self.ops[e]:
                if o.dres is not None:
                    lastq[id(o.dres)] = (e, o.dres)
        engobj = {"pe": "tensor", "act": "scalar", "dve": "vector", "pool": "gpsimd", "sp": "sync"}

        def run(e):
            def body(eng):
                waited = {}
                for o in self.ops[e]:
                    need = {}
                    for d in o.deps:
                        if d.dres is not None:
                            s, v = d.dres.sem, d.dval
                        else:
                            if d.eng == "pe" and e == "pe":
                                continue
                            s, v = esem[d.eng], d.val
                        k = id(s)
                        if waited.get(k, 0) >= v:
                            continue
                        if k not in need or need[k][1] < v:
                            need[k] = (s, v)
                    for k, (s, v) in need.items():
                        eng.wait_ge(s, v)
                        waited[k] = v
                    if o.dres is not None:
                        out, in_ = o.fn
                        eng.dma_start(out=out, in_=in_).then_inc(o.dres.sem, 16)
                    else:
                        ins = o.fn(eng)
                        if o.sig:
                            ins.then_inc(esem[e], 1)
                for k, (q, r) in lastq.items():
                    if q == e:
                        eng.wait_ge(r.sem, r.cnt)
            return body

        with nc.Block() as block:
            for e in ENG:
                if self.ops[e]:
                    getattr(block, engobj[e])(run(e))
        nc.all_engine_barrier()
        for r in self.dres:
            r.sem = None
            r.cnt = 0


class Buf:
    __slots__ = ("t", "r")

    def __init__(self, t, name, const=False):
        self.t = t
        self.r = Res(name, const)


class K:
    def __init__(self, depth=DEPTH, dbg=False):
        self.depth = depth
        self.dbg = dbg
        nc = self.nc = bass.Bass("TRN2", target_bir_lowering=False)
        L = depth

        def din(name, shape):
            return nc.dram_tensor(name, list(shape), F32, kind="ExternalInput").ap()

        self.x_in = din("x", [NT, D])
        self.carry = din("carry", [128, 1])
        self.w_in = din("w_in", [L, D, INC])
        self.w_gate = din("w_gate", [L, D, 3 * D])
        self.w_branch = din("w_branch", [L, 3, BW, D])
        self.w_out = din("w_out", [L, D, D])
        self.norm_gc = din("norm_gc", [L, 128, 32])
        self.fin_g = din("fin_g", [D])
        self.b_gatec = din("b_gatec", [L, 128, 96])
        self.sgu_lng = din("sgu_lng", [L, BW])
        self.sgu_wT = din("sgu_wT", [L, 128, 8, 128])
        self.sgu_b = din("sgu_b", [L, 1024])
        self.gla_wgk = din("gla_wgk", [L, 2, 17, 1024])
        self.gla_ng = din("gla_ng", [L, 512])
        self.na_bias = din("na_bias", [L, 3, 6, 16, 128, 256])
        self.y_out = nc.dram_tensor("y", [NT, D], F32, kind="ExternalOutput").ap()

        def scr(name, shape, dt):
            okind = "ExternalOutput" if (dbg and (dbg is True or name in dbg)) else "Internal"
            return nc.dram_tensor(name, list(shape), dt, kind=okind).ap()

        self.xs = scr("xs", [NT, D], F32)
        self.hT = scr("hT", [32, 128, NT], BF16)
        self.fms = scr("fms", [NFM, 128, NT], BF16)
        self.tms = scr("tms", [NT, NTM], BF16)
        self.yT = scr("yT", [48, 128, NT], BF16)
        self.og = scr("og", [2, NT, BW], F32)
        self.mT = scr("mT", [32, 128, NT], BF16)
        self.w_in_b = [nc.dram_tensor(f"w_in_b{i}", [D, INC], BF16).ap() for i in range(2)]
        self.wg_s = [nc.dram_tensor(f"wg_s{i}", [32, 3, 128, 32 * 128], BF16).ap() for i in range(2)]
        self.wb_s = [nc.dram_tensor(f"wb_s{i}", [32, 3, 128, 16 * 128], BF16).ap() for i in range(2)]
        self.w_out_b = [nc.dram_tensor(f"w_out_b{i}", [D, D], BF16).ap() for i in range(2)]
        self.nb_b = [nc.dram_tensor(f"nb_b{i}", [30, 16, 128, 256], BF16).ap() for i in range(2)]
        self.pid = 0

    def begin(self, name):
        self.pid += 1
        self.ph = Phase(self.nc, f"{name}{self.pid}")
        self._bn = 0
        return self.ph

    def sb(self, shape, dt, name=None, const=False):
        self._bn += 1
        nm = f"{self.ph.name}_{name or 'b'}{self._bn}"
        return Buf(self.nc.alloc_sbuf_tensor(nm, list(shape), dt), nm, const)

    def ps(self, shape=(128, 512), dt=F32, name=None):
        self._bn += 1
        nm = f"{self.ph.name}_{name or 'ps'}{self._bn}"
        return Buf(self.nc.alloc_psum_tensor(nm, list(shape), dt), nm)

    def consts(self, ph, need_ident=True):
        idf = self.sb([128, 128], F32, "idf")
        idb = self.sb([128, 128], BF16, "idb")
        ph.op("pool", lambda e: e.memset(idf.t[:], 0.0), writes=[idf.r])
        ph.op("pool", lambda e: e.affine_select(out=idf.t[:], in_=idf.t[:], pattern=[[-1, 128]],
                                                 compare_op=ALU.not_equal, fill=1.0, base=0,
                                                 channel_multiplier=1),
              reads=[idf.r], writes=[idf.r])
        ph.op("dve", lambda e: e.tensor_copy(out=idb.t[:], in_=idf.t[:]), reads=[idf.r], writes=[idb.r])
        idb.r.const = True
        return idf, idb

    def conv(self, ph, l, parts):
        if l >= self.depth:
            return
        q = l % 2
        k = Res("cv")
        if "in" in parts:
            for i in range(32):
                ph.dma("pool", self.w_in_b[q][i * 128:(i + 1) * 128, :], self.w_in[l, i * 128:(i + 1) * 128, :], k)
        if "gb" in parts:
            wg = self.w_gate[l].rearrange("(k p) (b f c) -> f b p k c", p=128, b=3, f=32)
            wb = self.w_branch[l].rearrange("b (k p) (f c) -> f b p k c", p=128, f=32)
            for f in range(32):
                for b in range(3):
                    ph.dma("pool", self.wg_s[q][f, b].rearrange("p (k c) -> p k c", k=32), wg[f, b], k)
                    ph.dma("pool", self.wb_s[q][f, b].rearrange("p (k c) -> p k c", k=16), wb[f, b], k)
        if "out" in parts:
            for i in range(16):
                ph.dma("pool", self.w_out_b[q][i * 256:(i + 1) * 256, :], self.w_out[l, i * 256:(i + 1) * 256, :], k)
            nb = self.na_bias[l].rearrange("a t h p q -> (a t) h p q")
            for i in range(18):
                ph.dma("pool", self.nb_b[q][i], nb[i], k)
            car = self.sb([128, 2], F32, "car", const=True)
            ph.dma("sp", car.t[:, 0:1], self.carry, car.r, writes=[car.r])
            ph.op("dve", lambda e: e.tensor_scalar(out=car.t[:, 1:2], in0=car.t[:, 0:1], scalar1=-1.0, scalar2=1.0,
                                                   op0=ALU.mult, op1=ALU.add), reads=[car.r], writes=[car.r])
            a = self.sb([128, 16, 256], F32, "ti")
            b_ = self.sb([128, 16, 256], F32, "te")
            o = self.sb([128, 16, 256], BF16, "to")
            for t in range(6):
                for (edge, slot) in ((2, 18), (0, 24)):
                    ph.dma("sp", a.t[:], self.na_bias[l, 1, t].rearrange("h p q -> p h q"), a.r, writes=[a.r])
                    ph.dma("sp", b_.t[:], self.na_bias[l, edge, t].rearrange("h p q -> p h q"), b_.r, writes=[b_.r])
                    ph.op("dve", lambda e: e.tensor_scalar(out=a.t[:], in0=a.t[:], scalar1=car.t[:, 0:1],
                                                           scalar2=None, op0=ALU.mult),
                          reads=[a.r, car.r], writes=[a.r])
                    ph.op("dve", lambda e: e.scalar_tensor_tensor(
                        out=o.t[:], in0=b_.t[:], scalar=car.t[:, 1:2], in1=a.t[:], op0=ALU.mult, op1=ALU.add),
                        reads=[a.r, b_.r, car.r], writes=[o.r])
                    ph.dma("pool", self.nb_b[q][slot + t].rearrange("h p q -> p h q"), o.t[:], o.r, reads=[o.r])

    def phase_convert(self, l):
        nc = self.nc
        with nc.cleanup_on_exit():
            ph = self.begin("cv")
            self.conv(ph, l, ("in", "gb", "out"))
            ph.emit()

    def phase_norm(self, l, final=False):
        nc = self.nc
        src = self.x_in if (l == 0 and not final) else self.xs
        with nc.cleanup_on_exit():
            ph = self.begin("nm")
            if not final:
                idf, idb = self.consts(ph)
                gcol = self.sb([128, 32], F32, "gcol", const=True)
                ph.dma("sp", gcol.t[:], self.norm_gc[l], gcol.r, writes=[gcol.r])
                hts = [self.sb([128, 32, TT], BF16, "hts") for _ in range(2)]
                pst = [self.ps([128, 8, 128], BF16, "pst") for _ in range(4)]
            else:
                gbc = self.sb([128, D], F32, "gbc", const=True)
                ph.dma("sp", gbc.t[:], self.fin_g.partition_broadcast(128), gbc.r, writes=[gbc.r])
                yts = [self.sb([128, D], F32, "yt") for _ in range(2)]
            xts = [self.sb([128, D], F32, "xt") for _ in range(2)]
            junk = self.sb([128, D], BF16, "junk")
            xns = [self.sb([128, D], BF16, "xn") for _ in range(2)]
            sts = [self.sb([128, 4], F32, "st") for _ in range(2)]
            for it in range(NT // 128):
                xt, xn, st = xts[it % 2], xns[it % 2], sts[it % 2]
                ph.dma("sp", xt.t[:], src[it * 128:(it + 1) * 128, :], xt.r, writes=[xt.r])
                if l == 0 and not final:
                    ph.dma("pool", self.xs[it * 128:(it + 1) * 128, :], xt.t[:], xt.r, reads=[xt.r])
                ph.op("act", lambda e, xt=xt, st=st: e.activation(out=junk.t[:], in_=xt.t[:], func=AF.Square,
                                                                    accum_out=st.t[:, 0:1]),
                      reads=[xt.r], writes=[junk.r, st.r])
                ph.op("dve", lambda e, st=st: e.tensor_scalar(out=st.t[:, 1:2], in0=st.t[:, 0:1], scalar1=1.0 / D,
                                                               scalar2=EPS, op0=ALU.mult, op1=ALU.add),
                      reads=[st.r], writes=[st.r])
                ph.op("act", lambda e, st=st: e.activation(out=st.t[:, 2:3], in_=st.t[:, 1:2], func=AF.Sqrt),
                      reads=[st.r], writes=[st.r])
                ph.op("dve", lambda e, st=st: e.reciprocal(out=st.t[:, 3:4], in_=st.t[:, 2:3]),
                      reads=[st.r], writes=[st.r])
                if final:
                    yt = yts[it % 2]
                    ph.op("dve", lambda e, xt=xt, st=st, yt=yt: e.scalar_tensor_tensor(
                        out=yt.t[:], in0=xt.t[:], scalar=st.t[:, 3:4], in1=gbc.t[:], op0=ALU.mult, op1=ALU.mult),
                        reads=[xt.r, st.r, gbc.r], writes=[yt.r])
                    ph.dma("pool", self.y_out[it * 128:(it + 1) * 128, :], yt.t[:], yt.r, reads=[yt.r])
                    continue
                ph.op("dve", lambda e, xt=xt, st=st, xn=xn: e.tensor_scalar(
                    out=xn.t[:], in0=xt.t[:], scalar1=st.t[:, 3:4], scalar2=None, op0=ALU.mult),
                    reads=[xt.r, st.r], writes=[xn.r])
                ht = hts[(it // 4) % 2]
                sub = it % 4
                for k8 in range(4):
                    pt = pst[k8]

                    def tr(e, xn=xn, pt=pt, k8=k8):
                        for j in range(8):
                            kk = k8 * 8 + j
                            ins = e.transpose(pt.t[:, j, :], xn.t[:, kk * 128:(kk + 1) * 128], idb.t[:])
                        return ins
                    ph.op("pe", tr, reads=[xn.r, idb.r], writes=[pt.r])

                    def ev(e, pt=pt, ht=ht, k8=k8, sub=sub):
                        for j in range(8):
                            kk = k8 * 8 + j
                            ins = e.tensor_scalar(out=ht.t[:, kk, sub * 128:(sub + 1) * 128], in0=pt.t[:, j, :],
                                                  scalar1=gcol.t[:, kk:kk + 1], scalar2=None, op0=ALU.mult)
                        return ins
                    ph.op("dve", ev, reads=[pt.r, gcol.r], writes=[ht.r])
                if sub == 3:
                    tt = it // 4
                    ph.dma("pool", self.hT[:, :, tt * TT:(tt + 1) * TT].rearrange("k p t -> p k t"), ht.t[:],
                           ht.r, reads=[ht.r])
            ph.emit()

    def phase_inproj(self, l):
        nc = self.nc
        jobs = []

        def add(c0, n, mode, epi, dest, scale=1.0):
            for b in range(0, n, 512):
                w = min(512, n - b)
                d = dest + (b // 128 if mode == "FM" else b)
                jobs.append((c0 + b, w, [(mode, epi, scale, d)]))
        add(0, 2048, "FM", "copy", FM_AU)
        add(2048, 2048, "TM", "copy", TM_AV)
        add(4096, 2048, "FM", "silu", FM_ASZ)
        add(6144, 1024, "FM", "scale", FM_BQ, 256 ** -0.5)
        for b in range(0, 1024, 512):
            jobs.append((7168 + b, 512, [("FM", "copy", 1.0, FM_BK + b // 128), ("TM", "copy", 1.0, TM_BK + b)]))
        add(8192, 2048, "TM", "copy", TM_BV)
        add(10240, 2048, "TM", "silu", TM_BSZ)
        add(12288, 32, "FM", "copy", FM_LR)
        add(12320, 2048, "FM", "scale", FM_CQ, 128 ** -0.5)
        add(14368, 2048, "FM", "copy", FM_CK)
        add(16416, 2048, "TM", "copy", TM_CV)
        add(18464, 2048, "FM", "silu", FM_CSZ)
        with nc.cleanup_on_exit():
            ph = self.begin("ip")
            hts = [self.sb([128, 32, TT], BF16, "ht") for _ in range(2)]
            wts = [self.sb([128, 32, 512], BF16, "wt") for _ in range(3)]
            pss = [self.ps(name="ps") for _ in range(6)]
            stg = [self.sb([128, 512], BF16, "stg") for _ in range(6)]
            wv = self.w_in_b[l % 2].rearrange("(k p) n -> p k n", p=128)
            nj = 0
            ne = 0
            for tt in range(NTT):
                ht = hts[tt % 2]
                ph.dma("sp", ht.t[:], self.hT[:, :, tt * TT:(tt + 1) * TT].rearrange("k p t -> p k t"), ht.r,
                       writes=[ht.r])
                for (c0, w, subs) in jobs:
                    wt = wts[nj % 3]
                    nj += 1
                    ph.dma("sp", wt.t[:, :, 0:w], wv[:, :, c0:c0 + w], wt.r, writes=[wt.r])
                    for (mode, epi, scale, dest) in subs:
                        nsub = (w + 127) // 128 if mode == "FM" else TT // 128
                        for s in range(nsub):
                            p = pss[ne % 6]
                            sg = stg[ne % 6]
                            ne += 1
                            if mode == "FM":
                                m = min(128, w - s * 128)

                                def mm(e, p=p, wt=wt, ht=ht, s=s, m=m):
                                    for k in range(32):
                                        ins = e.matmul(p.t[0:m, :], lhsT=wt.t[:, k, s * 128:s * 128 + m],
                                                       rhs=ht.t[:, k, :], start=(k == 0), stop=(k == 31))
                                    return ins
                                pv, sv = p.t[0:m, :], sg.t[0:m, :]
                                dst = self.fms[dest + s, 0:m, tt * TT:(tt + 1) * TT]
                            else:
                                def mm(e, p=p, wt=wt, ht=ht, s=s, w=w):
                                    for k in range(32):
                                        ins = e.matmul(p.t[:, 0:w], lhsT=ht.t[:, k, s * 128:(s + 1) * 128],
                                                       rhs=wt.t[:, k, 0:w], start=(k == 0), stop=(k == 31))
                                    return ins
                                pv, sv = p.t[:, 0:w], sg.t[:, 0:w]
                                dst = self.tms[tt * TT + s * 128:tt * TT + (s + 1) * 128, dest:dest + w]
                            ph.op("pe", mm, reads=[wt.r, ht.r], writes=[p.r])
                            if epi == "silu":
                                ph.op("act", lambda e, pv=pv, sv=sv: e.activation(out=sv, in_=pv, func=AF.Silu),
                                      reads=[p.r], writes=[sg.r])
                            elif epi == "scale":
                                ph.op("dve", lambda e, pv=pv, sv=sv, scale=scale: e.tensor_scalar(
                                    out=sv, in0=pv, scalar1=scale, scalar2=None, op0=ALU.mult),
                                    reads=[p.r], writes=[sg.r])
                            elif ne % 4 == 0:
                                ph.op("act", lambda e, pv=pv, sv=sv: e.activation(out=sv, in_=pv, func=AF.Copy),
                                      reads=[p.r], writes=[sg.r])
                            else:
                                ph.op("dve", lambda e, pv=pv, sv=sv: e.tensor_copy(out=sv, in_=pv),
                                      reads=[p.r], writes=[sg.r])
                            ph.dma("pool", dst, sv, sg.r, reads=[sg.r])
            ph.emit()

    def phase_sgu(self, l):
        nc = self.nc
        with nc.cleanup_on_exit():
            ph = self.begin("sg")
            lng = self.sb([128, BW], F32, "lng", const=True)
            ph.dma("sp", lng.t[:], self.sgu_lng[l].partition_broadcast(128), lng.r, writes=[lng.r])
            wsf = self.sb([128, 8, 128], F32, "wsf")
            ph.dma("sp", wsf.t[:], self.sgu_wT[l], wsf.r, writes=[wsf.r])
            wsb = self.sb([128, 8, 128], BF16, "wsb", const=True)
            ph.op("dve", lambda e: e.tensor_copy(out=wsb.t[:], in_=wsf.t[:]), reads=[wsf.r], writes=[wsb.r])
            bsb = self.sb([128, 8, 128], F32, "bsb", const=True)
            ph.dma("sp", bsb.t[:], self.sgu_b[l].partition_broadcast(128).rearrange("p (g i) -> p g i", g=8),
                   bsb.r, writes=[bsb.r])
            bs4 = self.sb([128, 8, 4, 128], F32, "bs4", const=True)

            def cpb(e):
                for c in range(4):
                    ins = e.tensor_copy(out=bs4.t[:, :, c, :], in_=bsb.t[:])
                return ins
            ph.op("dve", cpb, reads=[bsb.r], writes=[bs4.r])
            vts = [self.sb([128, 4, BW], BF16, "vt") for _ in range(2)]
            uts = [self.sb([128, 16, TT], BF16, "ut") for _ in range(2)]
            zts = [self.sb([128, 16, TT], BF16, "zt") for _ in range(2)]
            vn = self.sb([128, BW], F32, "vn")
            vnb = [self.sb([128, 4, BW], BF16, "vnb") for _ in range(1)]
            junk = self.sb([128, BW], BF16, "junk")
            sts = [self.sb([128, 8], F32, "st") for _ in range(2)]
            pss = [self.ps([128, 4, 128], F32, "ps") for _ in range(4)]
            t1s = [self.sb([128, 4, 128], F32, "t1") for _ in range(2)]
            yst = [self.sb([128, 16, TT], BF16, "ys") for _ in range(2)]
            nst = 0
            for tt in range(NTT):
                vt, ut, zt, vb, ys = vts[tt % 2], uts[tt % 2], zts[tt % 2], vnb[0], yst[tt % 2]
                t0 = tt * TT
                ph.dma("sp", vt.t[:], self.tms[t0:t0 + TT, TM_AV:TM_AV + BW].rearrange("(c p) n -> p c n", p=128),
                       vt.r, writes=[vt.r])
                ph.dma("sp", ut.t[:], self.fms[FM_AU:FM_AU + 16, :, t0:t0 + TT].rearrange("k p t -> p k t"),
                       ut.r, writes=[ut.r])
                ph.dma("sp", zt.t[:], self.fms[FM_ASZ:FM_ASZ + 16, :, t0:t0 + TT].rearrange("k p t -> p k t"),
                       zt.r, writes=[zt.r])
                for c in range(4):
                    st = sts[nst % 2]
                    nst += 1
                    vc = vt.t[:, c, :]
                    ph.op("dve", lambda e, vc=vc, st=st: e.reduce_sum(out=st.t[:, 0:1], in_=vc, axis=AX.X),
                          reads=[vt.r], writes=[st.r])
                    ph.op("act", lambda e, vc=vc, st=st: e.activation(out=junk.t[:], in_=vc, func=AF.Square,
                                                                       accum_out=st.t[:, 1:2]),
                          reads=[vt.r], writes=[junk.r, st.r])
                    ph.op("dve", lambda e, st=st: e.tensor_scalar(out=st.t[:, 2:3], in0=st.t[:, 0:1],
                                                                   scalar1=1.0 / BW, scalar2=None, op0=ALU.mult),
                          reads=[st.r], writes=[st.r])
                    ph.op("dve", lambda e, st=st: e.tensor_tensor(out=st.t[:, 3:4], in0=st.t[:, 2:3],
                                                                   in1=st.t[:, 2:3], op=ALU.mult),
                          reads=[st.r], writes=[st.r])
                    ph.op("dve", lambda e, st=st: e.scalar_tensor_tensor(
                        out=st.t[:, 4:5], in0=st.t[:, 1:2], scalar=1.0 / BW, in1=st.t[:, 3:4],
                        op0=ALU.mult, op1=ALU.subtract), reads=[st.r], writes=[st.r])
                    ph.op("dve", lambda e, st=st: e.tensor_scalar(out=st.t[:, 4:5], in0=st.t[:, 4:5], scalar1=EPS,
                                                                   scalar2=None, op0=ALU.add),
                          reads=[st.r], writes=[st.r])
                    ph.op("act", lambda e, st=st: e.activation(out=st.t[:, 5:6], in_=st.t[:, 4:5], func=AF.Sqrt),
                          reads=[st.r], writes=[st.r])
                    ph.op("dve", lambda e, st=st: e.reciprocal(out=st.t[:, 6:7], in_=st.t[:, 5:6]),
                          reads=[st.r], writes=[st.r])
                    ph.op("dve", lambda e, st=st: e.scalar_tensor_tensor(
                        out=st.t[:, 7:8], in0=st.t[:, 2:3], scalar=-1.0, in1=st.t[:, 6:7],
                        op0=ALU.mult, op1=ALU.mult), reads=[st.r], writes=[st.r])
                    ph.op("act", lambda e, vc=vc, st=st: e.activation(out=vn.t[:], in_=vc, func=AF.Identity,
                                                                       bias=st.t[:, 7:8], scale=st.t[:, 6:7]),
                          reads=[vt.r, st.r], writes=[vn.r])
                    ph.op("dve", lambda e, vb=vb, c=c: e.tensor_tensor(out=vb.t[:, c, :], in0=vn.t[:],
                                                                        in1=lng.t[:], op=ALU.mult),
                          reads=[vn.r, lng.r], writes=[vb.r])
                for cc in range(16):
                    g = cc // 2
                    p = pss[cc % 4]
                    t1 = t1s[cc % 2]

                    def mm(e, p=p, vb=vb, cc=cc, g=g):
                        for c in range(4):
                            ins = e.matmul(p.t[:, c, :], lhsT=vb.t[:, c, cc * 128:(cc + 1) * 128],
                                           rhs=wsb.t[:, g, :], start=True, stop=True)
                        return ins
                    ph.op("pe", mm, reads=[vb.r, wsb.r], writes=[p.r])
                    ph.op("dve", lambda e, p=p, t1=t1, g=g: e.tensor_tensor(
                        out=t1.t[:], in0=p.t[:], in1=bs4.t[:, g, :, :], op=ALU.add),
                        reads=[p.r, bs4.r], writes=[t1.r])
                    t1f = t1.t[:].rearrange("p c i -> p (c i)")
                    ph.op("dve", lambda e, t1f=t1f, ut=ut, cc=cc: e.tensor_tensor(
                        out=t1f, in0=t1f, in1=ut.t[:, cc, :], op=ALU.mult), reads=[t1.r, ut.r], writes=[t1.r])
                    ph.op("dve", lambda e, t1f=t1f, zt=zt, ys=ys, cc=cc: e.tensor_tensor(
                        out=ys.t[:, cc, :], in0=t1f, in1=zt.t[:, cc, :], op=ALU.mult),
                        reads=[t1.r, zt.r], writes=[ys.r])
                ph.dma("pool", self.yT[0:16, :, t0:t0 + TT].rearrange("k p t -> p k t"), ys.t[:], ys.r,
                       reads=[ys.r])
            ph.emit()

    def phase_gla(self, l, rev):
        nc = self.nc
        GT_ = 256
        ntile = NT // GT_
        d = 1 if rev else 0
        with nc.cleanup_on_exit():
            ph = self.begin("gb" if rev else "gf")
            if not rev:
                self.conv(ph, l + 1, ("in",))
            triA = self.sb([64, 64], F32, "triA")
            triB = self.sb([64, 64], F32, "triB")
            mskA = self.sb([64, 4, 64], F32, "mskA")
            sA = 1 if not rev else -1
            ph.op("pool", lambda e: e.memset(triA.t[:], -1.0 / 16), writes=[triA.r])
            ph.op("pool", lambda e: e.affine_select(out=triA.t[:], in_=triA.t[:], pattern=[[sA, 64]],
                                                     compare_op=ALU.is_ge, fill=0.0, base=0, channel_multiplier=-sA),
                  reads=[triA.r], writes=[triA.r])
            ph.op("pool", lambda e: e.memset(triB.t[:], -1.0 / 16), writes=[triB.r])
            ph.op("pool", lambda e: e.affine_select(out=triB.t[:], in_=triB.t[:], pattern=[[-sA, 64]],
                                                     compare_op=ALU.is_gt, fill=0.0, base=0, channel_multiplier=sA),
                  reads=[triB.r], writes=[triB.r])
            ph.op("pool", lambda e: e.memset(mskA.t[:], 1.0), writes=[mskA.r])
            ph.op("pool", lambda e: e.affine_select(out=mskA.t[:], in_=mskA.t[:], pattern=[[0, 4], [sA, 64]],
                                                     compare_op=ALU.is_ge, fill=0.0, base=0, channel_multiplier=-sA),
                  reads=[mskA.r], writes=[mskA.r])
            triA.r.const = triB.r.const = mskA.r.const = True
            wgf = self.sb([17, 1024], F32, "wgf")
            ph.dma("sp", wgf.t[:], self.gla_wgk[l, d], wgf.r, writes=[wgf.r])
            wgb = self.sb([17, 1024], BF16, "wgb", const=True)
            ph.op("dve", lambda e: e.tensor_copy(out=wgb.t[:], in_=wgf.t[:]), reads=[wgf.r], writes=[wgb.r])
            lrt = self.sb([17, NT], BF16, "lrt", const=True)
            ph.op("pool", lambda e: e.memset(lrt.t[:], 1.0), writes=[lrt.r])
            ph.dma("sp", lrt.t[0:16, :], self.fms[FM_LR, d * 16:(d + 1) * 16, :], lrt.r, reads=[lrt.r],
                   writes=[lrt.r])
            car = self.sb([128, 1], F32, "car", const=True)
            ph.dma("sp", car.t[:], self.carry, car.r, writes=[car.r])
            S = self.sb([128, 8, 512], F32, "S")
            Sb = self.sb([128, 8, 512], BF16, "Sb")
            ph.op("pool", lambda e: e.memset(S.t[:], 0.0), writes=[S.r])
            ph.op("pool", lambda e: e.memset(Sb.t[:], 0.0), writes=[Sb.r])
            qts = [self.sb([128, 8, GT_], BF16, "qt") for _ in range(2)]
            kts = [self.sb([128, 8, GT_], BF16, "kt") for _ in range(2)]
            kms = [self.sb([64, 4, 1024], BF16, "km") for _ in range(2)]
            vms = [self.sb([64, 4, BW], BF16, "vm") for _ in range(2)]
            ex2 = [self.sb([64, 1024], F32, "ex") for _ in range(2)]
            Lg2 = [self.sb([64, 1024], F32, "Lg") for _ in range(2)]
            Ep2 = [self.sb([128, 8, 64], F32, "Ep") for _ in range(2)]
            Em2 = [self.sb([128, 8, 64], F32, "Em") for _ in range(2)]
            Er2 = [self.sb([64, 1024], F32, "Er") for _ in range(2)]
            qe2 = [self.sb([128, 8, 64], BF16, "qe") for _ in range(2)]
            ke2 = [self.sb([128, 8, 64], BF16, "ke") for _ in range(2)]
            kk2 = [self.sb([64, 1024], BF16, "kk") for _ in range(2)]
            aTm2 = [self.sb([64, 4, 64], BF16, "aTm") for _ in range(2)]
            Sb2 = [Sb, self.sb([128, 8, 512], BF16, "Sb1")]
            nsb = [0]
            ost = [self.sb([64, BW], F32, "ost") for _ in range(2)]
            pX = [self.ps(name="pX") for _ in range(2)]
            pG = self.ps([128, 8, 64], F32, "pG")
            pA = self.ps([64, 4, 64], F32, "pA")
            pO = [self.ps(name="pO") for _ in range(2)]
            pU = [self.ps(name="pU") for _ in range(2)]
            fms, tms = self.fms, self.tms
            order = list(range(ntile))
            if rev:
                order = order[::-1]
            nch = 0
            for ti, tile in enumerate(order):
                t0 = tile * GT_
                qt, kt, km, vm = qts[ti % 2], kts[ti % 2], kms[ti % 2], vms[ti % 2]
                ph.dma("sp", qt.t[:], fms[FM_BQ:FM_BQ + 8, :, t0:t0 + GT_].rearrange("k p t -> p k t"), qt.r,
                       writes=[qt.r])
                ph.dma("sp", kt.t[:], fms[FM_BK:FM_BK + 8, :, t0:t0 + GT_].rearrange("k p t -> p k t"), kt.r,
                       writes=[kt.r])
                ph.dma("sp", km.t[:], tms[t0:t0 + GT_, TM_BK:TM_BK + 1024].rearrange("(c p) n -> p c n", p=64),
                       km.r, writes=[km.r])
                ph.dma("sp", vm.t[:], tms[t0:t0 + GT_, TM_BV:TM_BV + BW].rearrange("(c p) n -> p c n", p=64),
                       vm.r, writes=[vm.r])
                cs = [3, 2, 1, 0] if rev else [0, 1, 2, 3]
                for c in cs:
                    tk0 = t0 + c * 64
                    par = nch % 2
                    ex, Lg, Ep, Em, Er = ex2[par], Lg2[par], Ep2[par], Em2[par], Er2[par]
                    qe, ke, kk, aTm = qe2[par], ke2[par], kk2[par], aTm2[par]
                    if (not rev and tk0 == NT // 2) or (rev and tk0 == NT // 2 - 64):
                        ph.op("dve", lambda e: e.tensor_scalar(out=S.t[:], in0=S.t[:], scalar1=car.t[:, 0:1],
                                                               scalar2=None, op0=ALU.mult),
                              reads=[S.r, car.r], writes=[S.r])
                        sbc = Sb2[nsb[0] % 2]
                        ph.op("pool", lambda e, sbc=sbc: e.tensor_copy(out=sbc.t[:], in_=S.t[:]), reads=[S.r],
                              writes=[sbc.r])

                    def mm1(e, tk0=tk0):
                        for hf in range(2):
                            ins = e.matmul(pX[hf].t[0:64, :], lhsT=lrt.t[0:17, tk0:tk0 + 64],
                                           rhs=wgb.t[0:17, hf * 512:(hf + 1) * 512], start=True, stop=True)
                        return ins
                    ph.op("pe", mm1, reads=[lrt.r, wgb.r], writes=[pX[0].r, pX[1].r])

                    def a1(e, ex=ex):
                        for hf in range(2):
                            ins = e.activation(out=ex.t[:, hf * 512:(hf + 1) * 512], in_=pX[hf].t[0:64, :],
                                               func=AF.Exp, scale=-1.0)
                        return ins
                    ph.op("act", a1, reads=[pX[0].r, pX[1].r], writes=[ex.r])
                    ph.op("act", lambda e, Lg=Lg, ex=ex: e.activation(out=Lg.t[:], in_=ex.t[:], func=AF.Ln, bias=1.0),
                          reads=[ex.r], writes=[Lg.r])

                    def mm2(e, Lg=Lg):
                        for dc in range(8):
                            ins = e.matmul(pG.t[:, dc, :], lhsT=Lg.t[:, dc * 128:(dc + 1) * 128], rhs=triA.t[:],
                                           start=True, stop=True)
                        for hf in range(2):
                            ins = e.matmul(pX[hf].t[0:64, :], lhsT=triB.t[:], rhs=Lg.t[:, hf * 512:(hf + 1) * 512],
                                           start=True, stop=True)
                        return ins
                    ph.op("pe", mm2, reads=[Lg.r, triA.r, triB.r, ex.r], writes=[pG.r, pX[0].r, pX[1].r])
                    ph.op("act", lambda e, Ep=Ep: e.activation(out=Ep.t[:], in_=pG.t[:], func=AF.Exp),
                          reads=[pG.r], writes=[Ep.r])
                    ph.op("act", lambda e, Em=Em: e.activation(out=Em.t[:], in_=pG.t[:], func=AF.Exp, scale=-1.0),
                          reads=[pG.r], writes=[Em.r])

                    def a3(e, Er=Er):
                        for hf in range(2):
                            ins = e.activation(out=Er.t[:, hf * 512:(hf + 1) * 512], in_=pX[hf].t[0:64, :],
                                               func=AF.Exp)
                        return ins
                    ph.op("act", a3, reads=[pX[0].r, pX[1].r], writes=[Er.r])
                    ph.op("dve", lambda e, qt=qt, c=c, qe=qe, Ep=Ep: e.tensor_tensor(
                        out=qe.t[:], in0=qt.t[:, :, c * 64:(c + 1) * 64], in1=Ep.t[:], op=ALU.mult),
                        reads=[qt.r, Ep.r], writes=[qe.r])
                    ph.op("dve", lambda e, kt=kt, c=c, ke=ke, Em=Em: e.tensor_tensor(
                        out=ke.t[:], in0=kt.t[:, :, c * 64:(c + 1) * 64], in1=Em.t[:], op=ALU.mult),
                        reads=[kt.r, Em.r], writes=[ke.r])
                    ph.op("dve", lambda e, km=km, c=c, kk=kk, Er=Er: e.tensor_tensor(
                        out=kk.t[:], in0=km.t[:, c, :], in1=Er.t[:], op=ALU.mult),
                        reads=[km.r, Er.r], writes=[kk.r])

                    def mm3(e, ke=ke, qe=qe):
                        for h in range(4):
                            for q in range(2):
                                dc = 2 * h + q
                                ins = e.matmul(pA.t[:, h, :], lhsT=ke.t[:, dc, :], rhs=qe.t[:, dc, :],
                                               start=(q == 0), stop=(q == 1))
                        return ins
                    ph.op("pe", mm3, reads=[ke.r, qe.r], writes=[pA.r])
                    ph.op("dve", lambda e, aTm=aTm: e.tensor_tensor(out=aTm.t[:], in0=pA.t[:], in1=mskA.t[:], op=ALU.mult),
                          reads=[pA.r, mskA.r], writes=[aTm.r])
                    o_s = ost[nch % 2]
                    nch += 1
                    for h in range(4):
                        po = pO[h % 2]

                        def mm4(e, h=h, po=po, vm=vm, c=c, aTm=aTm, qe=qe, Sb=Sb2[nsb[0] % 2]):
                            e.matmul(po.t[0:64, :], lhsT=aTm.t[:, h, :], rhs=vm.t[:, c, h * 512:(h + 1) * 512],
                                     start=True, stop=False)
                            e.matmul(po.t[0:64, :], lhsT=qe.t[:, 2 * h, :], rhs=Sb.t[:, 2 * h, :],
                                     start=False, stop=False)
                            return e.matmul(po.t[0:64, :], lhsT=qe.t[:, 2 * h + 1, :], rhs=Sb.t[:, 2 * h + 1, :],
                                            start=False, stop=True)
                        ph.op("pe", mm4, reads=[aTm.r, vm.r, qe.r, Sb2[nsb[0] % 2].r], writes=[po.r])
                        ph.op("act", lambda e, h=h, po=po, o_s=o_s: e.activation(
                            out=o_s.t[:, h * 512:(h + 1) * 512], in_=po.t[0:64, :], func=AF.Copy),
                            reads=[po.r], writes=[o_s.r])
                    ph.dma("pool", self.og[d, tk0:tk0 + 64, :], o_s.t[:], o_s.r, reads=[o_s.r])
                    gl = 0 if rev else 63
                    for dc in range(8):
                        pu = pU[dc % 2]
                        h = dc // 2

                        def mm5(e, dc=dc, pu=pu, h=h, vm=vm, c=c, kk=kk):
                            return e.matmul(pu.t[:], lhsT=kk.t[:, dc * 128:(dc + 1) * 128],
                                            rhs=vm.t[:, c, h * 512:(h + 1) * 512], start=True, stop=True)
                        ph.op("pe", mm5, reads=[kk.r, vm.r], writes=[pu.r])
                        ph.op("dve", lambda e, dc=dc, pu=pu, gl=gl, Ep=Ep: e.scalar_tensor_tensor(
                            out=S.t[:, dc, :], in0=S.t[:, dc, :], scalar=Ep.t[:, dc, gl:gl + 1], in1=pu.t[:],
                            op0=ALU.mult, op1=ALU.add), reads=[S.r, Ep.r, pu.r], writes=[S.r])
                    nsb[0] += 1
                    sbn = Sb2[nsb[0] % 2]
                    ph.op("pool", lambda e, sbn=sbn: e.tensor_copy(out=sbn.t[:], in_=S.t[:]), reads=[S.r],
                          writes=[sbn.r])
            ph.emit()

    def phase_gla_out(self, l):
        nc = self.nc
        with nc.cleanup_on_exit():
            ph = self.begin("go")
            self.conv(ph, l + 1, ("out",))
            idf, idb = self.consts(ph)
            ngb = self.sb([128, 512], F32, "ngb", const=True)
            ph.dma("sp", ngb.t[:], self.gla_ng[l].partition_broadcast(128), ngb.r, writes=[ngb.r])
            ofs = [self.sb([128, BW], F32, "of") for _ in range(2)]
            obs = [self.sb([128, BW], F32, "ob") for _ in range(2)]
            szs = [self.sb([128, BW], BF16, "sz") for _ in range(2)]
            junk = self.sb([128, 512], BF16, "junk")
            sts = [self.sb([128, 16], F32, "st") for _ in range(2)]
            yb = [self.sb([128, BW], BF16, "yb") for _ in range(2)]
            pst = [self.ps([128, 8, 128], BF16, "pst") for _ in range(2)]
            yts = [self.sb([128, 16, TT], BF16, "yt") for _ in range(2)]
            for it in range(NT // 128):
                of, ob, sz, st, y = ofs[it % 2], obs[it % 2], szs[it % 2], sts[it % 2], yb[it % 2]
                r0 = it * 128
                ph.dma("sp", of.t[:], self.og[0, r0:r0 + 128, :], of.r, writes=[of.r])
                ph.dma("sp", ob.t[:], self.og[1, r0:r0 + 128, :], ob.r, writes=[ob.r])
                ph.dma("sp", sz.t[:], self.tms[r0:r0 + 128, TM_BSZ:TM_BSZ + BW], sz.r, writes=[sz.r])
                ph.op("dve", lambda e, of=of, ob=ob: e.tensor_tensor(out=of.t[:], in0=of.t[:], in1=ob.t[:],
                                                                     op=ALU.add),
                      reads=[of.r, ob.r], writes=[of.r])

                def sq(e, of=of, st=st):
                    for h in range(4):
                        ins = e.activation(out=junk.t[:], in_=of.t[:, h * 512:(h + 1) * 512], func=AF.Square,
                                           accum_out=st.t[:, h:h + 1])
                    return ins
                ph.op("act", sq, reads=[of.r], writes=[junk.r, st.r])
                ph.op("dve", lambda e, st=st: e.tensor_scalar(out=st.t[:, 4:8], in0=st.t[:, 0:4], scalar1=1.0 / 512,
                                                               scalar2=EPS, op0=ALU.mult, op1=ALU.add),
                      reads=[st.r], writes=[st.r])
                ph.op("act", lambda e, st=st: e.activation(out=st.t[:, 8:12], in_=st.t[:, 4:8], func=AF.Sqrt),
                      reads=[st.r], writes=[st.r])
                ph.op("dve", lambda e, st=st: e.reciprocal(out=st.t[:, 12:16], in_=st.t[:, 8:12]),
                      reads=[st.r], writes=[st.r])

                def nrm(e, of=of, st=st):
                    for h in range(4):
                        ins = e.scalar_tensor_tensor(out=of.t[:, h * 512:(h + 1) * 512],
                                                     in0=of.t[:, h * 512:(h + 1) * 512],
                                                     scalar=st.t[:, 12 + h:13 + h], in1=ngb.t[:],
                                                     op0=ALU.mult, op1=ALU.mult)
                    return ins
                ph.op("dve", nrm, reads=[of.r, st.r, ngb.r], writes=[of.r])
                ph.op("dve", lambda e, of=of, sz=sz, y=y: e.tensor_tensor(out=y.t[:], in0=of.t[:], in1=sz.t[:],
                                                                           op=ALU.mult),
                      reads=[of.r, sz.r], writes=[y.r])
                yt = yts[(it // 4) % 2]
                sub = it % 4
                for k8 in range(2):
                    pt = pst[k8]

                    def tr(e, y=y, pt=pt, k8=k8):
                        for j in range(8):
                            kk_ = k8 * 8 + j
                            ins = e.transpose(pt.t[:, j, :], y.t[:, kk_ * 128:(kk_ + 1) * 128], idb.t[:])
                        return ins
                    ph.op("pe", tr, reads=[y.r, idb.r], writes=[pt.r])
                    ph.op("act", lambda e, pt=pt, yt=yt, k8=k8, sub=sub: e.activation(
                        out=yt.t[:, k8 * 8:(k8 + 1) * 8, sub * 128:(sub + 1) * 128], in_=pt.t[:], func=AF.Copy),
                        reads=[pt.r], writes=[yt.r])
                if sub == 3:
                    t0 = (it // 4) * TT
                    ph.dma("pool", self.yT[16:32, :, t0:t0 + TT].rearrange("k p t -> p k t"), yt.t[:], yt.r,
                           reads=[yt.r])
            ph.emit()

    def phase_na(self, l):
        nc = self.nc
        NB = NT // 256
        with nc.cleanup_on_exit():
            ph = self.begin("na")
            self.conv(ph, l + 1, ("gb",))
            idf, idb = self.consts(ph)
            ones = self.sb([128, 128], BF16, "ones", const=True)
            ph.op("pool", lambda e: e.memset(ones.t[:], 1.0), writes=[ones.r])
            qts = [self.sb([128, NT], BF16, "q") for _ in range(2)]
            kts = [self.sb([128, NT], BF16, "k") for _ in range(2)]
            vts = [self.sb([128, 32, 128], BF16, "v") for _ in range(2)]
            zts = [self.sb([128, NT], BF16, "z") for _ in range(2)]
            bts = [self.sb([128, 30, 256], BF16, "b") for _ in range(2)]
            yts = [self.sb([128, NT], BF16, "y") for _ in range(2)]
            pS = [self.ps([128, 256], F32, "pS") for _ in range(3)]
            pO = [self.ps([128, 256], F32, "pO") for _ in range(2)]
            pD = [self.ps([128, 256], F32, "pD") for _ in range(2)]
            pts = [self.sb([128, 256], BF16, "pt") for _ in range(3)]
            rd = [self.sb([128, 256], F32, "rd") for _ in range(2)]
            o1 = [self.sb([128, 256], F32, "o1") for _ in range(2)]
            nS = 0
            nB = 0
            for h in range(16):
                qt, kt, vt, zt, bt, yt = qts[h % 2], kts[h % 2], vts[h % 2], zts[h % 2], bts[h % 2], yts[h % 2]
                ph.dma("sp", qt.t[:], self.fms[FM_CQ + h], qt.r, writes=[qt.r])
                ph.dma("sp", kt.t[:], self.fms[FM_CK + h], kt.r, writes=[kt.r])
                ph.dma("sp", zt.t[:], self.fms[FM_CSZ + h], zt.r, writes=[zt.r])
                ph.dma("sp", vt.t[:], self.tms[:, TM_CV + h * 128:TM_CV + (h + 1) * 128].rearrange(
                    "(c p) n -> p c n", p=128), vt.r, writes=[vt.r])
                ph.dma("sp", bt.t[:], self.nb_b[l % 2][:, h].rearrange("a p q -> p a q"), bt.r, writes=[bt.r])
                for b in range(NB):
                    kind = {0: 0, NB - 1: 2, NB // 2 - 1: 3, NB // 2: 4}.get(b, 1)
                    tl = [t for t in range(6) if 0 <= 4 * b - 4 + 2 * t and 4 * b - 4 + 2 * t + 2 <= NT // 64]
                    po, pd = pO[nB % 2], pD[nB % 2]
                    q0 = b * 256
                    slots = []
                    for t in tl:
                        slots.append((pS[nS % 3], pts[nS % 3]))
                        nS += 1

                    def do_mms(ix):
                        t = tl[ix]
                        kc = (4 * b - 4 + 2 * t) // 2
                        p_s = slots[ix][0]

                        def mms(e, p_s=p_s, kt=kt, qt=qt, bt=bt, kc=kc, q0=q0, kind=kind, t=t):
                            e.matmul(p_s.t[:], lhsT=kt.t[:, kc * 128:(kc + 1) * 128], rhs=qt.t[:, q0:q0 + 256],
                                     start=True, stop=False)
                            return e.matmul(p_s.t[:], lhsT=idb.t[:], rhs=bt.t[:, kind * 6 + t, :],
                                            start=False, stop=True)
                        ph.op("pe", mms, reads=[kt.r, qt.r, bt.r, idb.r], writes=[p_s.r])

                    def do_rest(ix):
                        t = tl[ix]
                        kc = (4 * b - 4 + 2 * t) // 2
                        p_s, pt = slots[ix]
                        ph.op("act", lambda e, p_s=p_s, pt=pt: e.activation(out=pt.t[:], in_=p_s.t[:], func=AF.Exp),
                              reads=[p_s.r], writes=[pt.r])

                        def mmo(e, po=po, pd=pd, vt=vt, pt=pt, kc=kc, first=(ix == 0), last=(ix == len(tl) - 1)):
                            e.matmul(po.t[:], lhsT=vt.t[:, kc, :], rhs=pt.t[:], start=first, stop=last)
                            return e.matmul(pd.t[:], lhsT=ones.t[:], rhs=pt.t[:], start=first, stop=last)
                        ph.op("pe", mmo, reads=[vt.r, pt.r, ones.r], writes=[po.r, pd.r])
                    do_mms(0)
                    for ix in range(len(tl)):
                        if ix + 1 < len(tl):
                            do_mms(ix + 1)
                        do_rest(ix)
                    r_, o_ = rd[nB % 2], o1[nB % 2]
                    nB += 1
                    ph.op("dve", lambda e, r_=r_, pd=pd: e.reciprocal(out=r_.t[:], in_=pd.t[:]),
                          reads=[pd.r], writes=[r_.r])
                    ph.op("dve", lambda e, o_=o_, po=po, r_=r_: e.tensor_tensor(out=o_.t[:], in0=po.t[:],
                                                                                   in1=r_.t[:], op=ALU.mult),
                          reads=[po.r, r_.r], writes=[o_.r])
                    ph.op("dve", lambda e, o_=o_, zt=zt, yt=yt, q0=q0: e.tensor_tensor(
                        out=yt.t[:, q0:q0 + 256], in0=o_.t[:], in1=zt.t[:, q0:q0 + 256], op=ALU.mult),
                        reads=[o_.r, zt.r], writes=[yt.r])
                ph.dma("pool", self.yT[32 + h], yt.t[:], yt.r, reads=[yt.r])
            ph.emit()

    def phase_merge(self, l):
        nc = self.nc
        with nc.cleanup_on_exit():
            ph = self.begin("mg")
            bg = self.sb([128, 96], F32, "bg", const=True)
            ph.dma("sp", bg.t[:], self.b_gatec[l], bg.r, writes=[bg.r])
            ht = self.sb([128, 32, TT], BF16, "ht")
            yt = self.sb([128, 48, TT], BF16, "yt")
            wgs = [self.sb([128, 32, 128], BF16, "wg") for _ in range(3)]
            wbs = [self.sb([128, 16, 128], BF16, "wb") for _ in range(3)]
            pG = [self.ps(name="pG") for _ in range(4)]
            pB = [self.ps(name="pB") for _ in range(4)]
            gs = [self.sb([128, TT], F32, "g") for _ in range(4)]
            ms = [self.sb([128, TT], F32, "m") for _ in range(2)]
            mts = [self.sb([128, 32, TT], BF16, "mt") for _ in range(1)]
            n3 = 0
            nf = 0
            for tt in range(NTT):
                t0 = tt * TT
                mt = mts[0]
                ph.dma("sp", ht.t[:], self.hT[:, :, t0:t0 + TT].rearrange("k p t -> p k t"), ht.r, writes=[ht.r])
                ph.dma("sp", yt.t[:], self.yT[:, :, t0:t0 + TT].rearrange("k p t -> p k t"), yt.r, writes=[yt.r])
                for f in range(32):
                    m = ms[nf % 2]
                    nf += 1
                    for b in range(3):
                        pg, pb, g = pG[n3 % 4], pB[n3 % 4], gs[n3 % 4]
                        wg, wb = wgs[n3 % 3], wbs[n3 % 3]
                        n3 += 1
                        ph.dma("sp", wg.t[:], self.wg_s[l % 2][f, b].rearrange("p (k c) -> p k c", k=32), wg.r,
                               writes=[wg.r])
                        ph.dma("sp", wb.t[:], self.wb_s[l % 2][f, b].rearrange("p (k c) -> p k c", k=16), wb.r,
                               writes=[wb.r])

                        def mmg(e, pg=pg, wg=wg, b=b):
                            for k in range(32):
                                ins = e.matmul(pg.t[:], lhsT=wg.t[:, k, :], rhs=ht.t[:, k, :],
                                               start=(k == 0), stop=(k == 31))
                            return ins
                        ph.op("pe", mmg, reads=[wg.r, ht.r], writes=[pg.r])

                        def mmb(e, pb=pb, wb=wb, b=b):
                            for k in range(16):
                                ins = e.matmul(pb.t[:], lhsT=wb.t[:, k, :], rhs=yt.t[:, b * 16 + k, :],
                                               start=(k == 0), stop=(k == 15))
                            return ins
                        ph.op("pe", mmb, reads=[wb.r, yt.r], writes=[pb.r])
                        col = b * 32 + f
                        ph.op("act", lambda e, pg=pg, g=g, col=col: e.activation(
                            out=g.t[:], in_=pg.t[:], func=AF.Sigmoid, bias=bg.t[:, col:col + 1]),
                            reads=[pg.r, bg.r], writes=[g.r])
                        if b == 0:
                            ph.op("dve", lambda e, m=m, g=g, pb=pb: e.tensor_tensor(out=m.t[:], in0=g.t[:],
                                                                                     in1=pb.t[:], op=ALU.mult),
                                  reads=[g.r, pb.r], writes=[m.r])
                        else:
                            ph.op("dve", lambda e, g=g, pb=pb: e.tensor_tensor(out=g.t[:], in0=g.t[:], in1=pb.t[:],
                                                                                op=ALU.mult),
                                  reads=[g.r, pb.r], writes=[g.r])
                            if b == 1:
                                ph.op("pool", lambda e, m=m, g=g: e.tensor_tensor(out=m.t[:], in0=m.t[:],
                                                                                   in1=g.t[:], op=ALU.add),
                                      reads=[m.r, g.r], writes=[m.r])
                            else:
                                ph.op("pool", lambda e, m=m, g=g, mt=mt, f=f: e.tensor_tensor(
                                    out=mt.t[:, f, :], in0=m.t[:], in1=g.t[:], op=ALU.add),
                                    reads=[m.r, g.r], writes=[mt.r])
                ph.dma("pool", self.mT[:, :, t0:t0 + TT].rearrange("k p t -> p k t"), mt.t[:], mt.r, reads=[mt.r])
            ph.emit()

    def phase_out(self, l):
        nc = self.nc
        with nc.cleanup_on_exit():
            ph = self.begin("op")
            mts = [self.sb([128, 32, TT], BF16, "mt") for _ in range(2)]
            wos = [self.sb([128, 32, 512], BF16, "wo") for _ in range(2)]
            pss = [self.ps(name="ps") for _ in range(4)]
            xss = [self.sb([128, 512], F32, "xs") for _ in range(4)]
            wv = self.w_out_b[l % 2].rearrange("(k p) n -> p k n", p=128)
            nw = 0
            ne = 0
            for tt in range(NTT):
                t0 = tt * TT
                mt = mts[tt % 2]
                ph.dma("sp", mt.t[:], self.mT[:, :, t0:t0 + TT].rearrange("k p t -> p k t"), mt.r, writes=[mt.r])
                for n in range(8):
                    wo = wos[nw % 2]
                    nw += 1
                    ph.dma("sp", wo.t[:], wv[:, :, n * 512:(n + 1) * 512], wo.r, writes=[wo.r])
                    for s in range(4):
                        p, xs_ = pss[ne % 4], xss[ne % 4]
                        ne += 1
                        rows = slice(t0 + s * 128, t0 + (s + 1) * 128)
                        cols = slice(n * 512, (n + 1) * 512)
                        ph.dma("sp", xs_.t[:], self.xs[rows, cols], xs_.r, writes=[xs_.r])

                        def mm(e, p=p, mt=mt, wo=wo, s=s):
                            for k in range(32):
                                ins = e.matmul(p.t[:], lhsT=mt.t[:, k, s * 128:(s + 1) * 128], rhs=wo.t[:, k, :],
                                               start=(k == 0), stop=(k == 31))
                            return ins
                        ph.op("pe", mm, reads=[mt.r, wo.r], writes=[p.r])
                        ph.op("dve", lambda e, p=p, xs_=xs_: e.tensor_tensor(out=xs_.t[:], in0=xs_.t[:], in1=p.t[:],
                                                                              op=ALU.add),
                              reads=[p.r, xs_.r], writes=[xs_.r])
                        ph.dma("pool", self.xs[rows, cols], xs_.t[:], xs_.r, reads=[xs_.r])
            ph.emit()

    def build(self, upto=None):
        self.phase_convert(0)
        for l in range(self.depth):
            self.phase_norm(l)
            self.phase_inproj(l)
            self.phase_sgu(l)
            self.phase_gla(l, False)
            self.phase_gla(l, True)
            self.phase_gla_out(l)
            self.phase_na(l)
            self.phase_merge(l)
            self.phase_out(l)
        self.phase_norm(self.depth, final=True)
        return self.nc


def _na_bias_tiles(rpb_l):
    H = rpb_l.shape[0]
    cols = np.arange(64)
    cs = np.clip(cols - 8, 0, 48)
    col_ok = (cols[None, :] >= cs[:, None]) & (cols[None, :] < cs[:, None] + 16)
    dc = np.clip(cols[None, :] - cols[:, None] + 15, 0, 30)
    out = np.full((3, 6, H, 128, 256), NEG, np.float32)
    for kind in range(3):
        for t in range(6):
            for kr2 in range(2):
                ko = -4 + 2 * t + kr2
                for qr in range(4):
                    krel = ko - qr
                    if kind == 1:
                        ok = -4 <= krel <= 3
                    elif kind == 0:
                        ok = 0 <= ko <= 7
                    else:
                        ok = -4 <= ko <= 3
                    if not ok:
                        continue
                    blk = rpb_l[:, krel + 7, :][:, dc]
                    blk = np.where(col_ok[None], blk, np.float32(NEG))
                    out[kind, t, :, kr2 * 64:(kr2 + 1) * 64, qr * 64:(qr + 1) * 64] = blk.transpose(0, 2, 1)
    return out


_NC_CACHE = {}


def _layout(inputs, depth=DEPTH):
    f32 = np.float32
    xp = np.asarray(inputs["x_prompt"], f32)
    xsmp = np.asarray(inputs["x_sample"], f32)
    slabs = [xsmp[0], xsmp[1],
             np.concatenate([xp[0], xp[1]], 0), np.concatenate([xp[2], xp[3]], 0)]
    zero = np.zeros_like(xp[0])
    for i in range(4, 8):
        slabs.append(np.concatenate([xp[i], zero], 0))
    two_seq = [False, False] + [True] * 6
    L = depth
    shared = {
        "w_in": np.ascontiguousarray(np.asarray(inputs["w_in"], f32)[:L]),
        "w_gate": np.ascontiguousarray(np.asarray(inputs["w_gate"], f32)[:L]),
        "w_branch": np.ascontiguousarray(np.asarray(inputs["w_branch"], f32)[:L]),
        "w_out": np.ascontiguousarray(np.asarray(inputs["w_out"], f32)[:L]),
        "norm_gc": np.ascontiguousarray(np.asarray(inputs["norm_g"], f32)[:L].reshape(L, 32, 128).transpose(0, 2, 1)),
        "fin_g": np.asarray(inputs["final_norm_g"], f32),
        "b_gatec": np.ascontiguousarray(np.asarray(inputs["b_gate"], f32)[:L].reshape(L, 96, 128).transpose(0, 2, 1)),
        "sgu_lng": np.ascontiguousarray(np.asarray(inputs["sgu_ln_g"], f32)[:L]),
        "sgu_wT": np.ascontiguousarray(np.asarray(inputs["sgu_w"], f32)[:L].transpose(0, 3, 1, 2)),
        "sgu_b": np.ascontiguousarray(np.asarray(inputs["sgu_b"], f32)[:L].reshape(L, 1024)),
        "gla_wgk": np.ascontiguousarray(np.concatenate(
            [np.asarray(inputs["gla_w_gk"], f32)[:L], np.asarray(inputs["gla_b_gk"], f32)[:L, :, None, :]], axis=2)),
        "gla_ng": np.ascontiguousarray(np.asarray(inputs["gla_norm_g"], f32)[:L]),
    }
    rpb = np.asarray(inputs["na_rpb"], f32)[:L]
    shared["na_bias"] = np.stack([_na_bias_tiles(rpb[l]) for l in range(L)])
    in_maps = []
    for c in range(8):
        m = dict(shared)
        m["x"] = np.ascontiguousarray(slabs[c])
        m["carry"] = np.full((128, 1), 0.0 if two_seq[c] else 1.0, f32)
        in_maps.append(m)
    return in_maps


def kernel(**inputs):
    if "nc" not in _NC_CACHE:
        _NC_CACHE["nc"] = K().build()
    nc = _NC_CACHE["nc"]
    in_maps = _layout(inputs)
    res = run_bass_kernel_spmd(nc, in_maps, core_ids=list(range(8)))
    ys = [np.asarray(r["y"]) for r in res.results]
    y_sample = np.stack([ys[0], ys[1]], 0)
    y_prompt = np.stack([ys[2][:2048], ys[2][2048:], ys[3][:2048], ys[3][2048:],
                         ys[4][:2048], ys[5][:2048], ys[6][:2048], ys[7][:2048]], 0)
    return (y_prompt.astype(np.float32), y_sample.astype(np.float32))
```

```python
import numpy as np
import concourse.bass as bass
import concourse.mybir as mybir
from concourse.bass_utils import run_bass_kernel_spmd

F32 = mybir.dt.float32
BF16 = mybir.dt.bfloat16
AF = mybir.ActivationFunctionType
ALU = mybir.AluOpType
AX = mybir.AxisListType
ENG = ("pe", "act", "dve", "pool", "sp")

D = 4096
DEPTH = 4
NT = 4096
TT = 512
NTT = NT // TT
INC = 20512
BW = 2048
EPS = 1e-6
NEG = -30000.0

FM_AU, FM_ASZ, FM_BQ, FM_BK, FM_LR, FM_CQ, FM_CK, FM_CSZ, NFM = 0, 16, 32, 40, 48, 49, 65, 81, 97
TM_AV, TM_BK, TM_BV, TM_BSZ, TM_CV, NTM = 0, 2048, 3072, 5120, 7168, 9216


class Res:
    __slots__ = ("name", "lw", "rd", "sem", "cnt", "const")

    def __init__(self, name, const=False):
        self.name = name
        self.lw = None
        self.rd = {}
        self.sem = None
        self.cnt = 0
        self.const = const


class Op:
    __slots__ = ("eng", "fn", "deps", "sig", "val", "dres", "dval")

    def __init__(self, eng, fn):
        self.eng = eng
        self.fn = fn
        self.deps = []
        self.sig = False
        self.val = 0
        self.dres = None
        self.dval = 0


class Phase:
    def __init__(self, nc, name):
        self.nc = nc
        self.name = name
        self.ops = {e: [] for e in ENG}
        self.dres = []

    def _track(self, o, reads, writes):
        deps = o.deps
        for r in reads:
            if r.lw is not None:
                deps.append(r.lw)
        for r in writes:
            if r.lw is not None:
                deps.append(r.lw)
            deps.extend(r.rd.values())
        key = o.eng if o.dres is None else ("d", id(o.dres))
        for r in reads:
            if not r.const:
                r.rd[key] = o
        for r in writes:
            r.lw = o
            r.rd = {}
        for d in deps:
            d.sig = True
        self.ops[o.eng].append(o)
        return o

    def op(self, eng, fn, reads=(), writes=()):
        return self._track(Op(eng, fn), reads, writes)

    def dma(self, q, out, in_, key, reads=(), writes=()):
        o = Op(q, (out, in_))
        if key.sem is None:
            self.dres.append(key)
            key.sem = True
        key.cnt += 16
        o.dres = key
        o.dval = key.cnt
        return self._track(o, reads, writes)

    def emit(self):
        nc = self.nc
        esem = {e: nc.alloc_semaphore(f"{self.name}_{e}") for e in ENG}
        for i, r in enumerate(self.dres):
            r.sem = nc.alloc_semaphore(f"{self.name}_d{i}")
        for e in ENG:
            c = 0
            for o in self.ops[e]:
                if o.dres is None and o.sig:
                    c += 1
                    o.val = c
        lastq = {}
        for e in ENG:
            for o in self.ops[e]:
                if o.dres is not None:
                    lastq[id(o.dres)] = (e, o.dres)
        engobj = {"pe": "tensor", "act": "scalar", "dve": "vector", "pool": "gpsimd", "sp": "sync"}

        def run(e):
            def body(eng):
                waited = {}
                for o in self.ops[e]:
                    need = {}
                    for d in o.deps:
                        if d.dres is not None:
                            s, v = d.dres.sem, d.dval
                        else:
                            if d.eng == "pe" and e == "pe":
                                continue
                            s, v = esem[d.eng], d.val
                        k = id(s)
                        if waited.get(k, 0) >= v:
                            continue
                        if k not in need or need[k][1] < v:
                            need[k] = (s, v)
                    for k, (s, v) in need.items():
                        eng.wait_ge(s, v)
                        waited[k] = v
                    if o.dres is not None:
                        out, in_ = o.fn
                        eng.dma_start(out=out, in_=in_).then_inc(o.dres.sem, 16)
                    else:
                        ins = o.fn(eng)
                        if o.sig:
                            ins.then_inc(esem[e], 1)
                for k, (q, r) in lastq.items():
                    if q == e:
                        eng.wait_ge(r.sem, r.cnt)
            return body

        with nc.Block() as block:
            for e in ENG:
                if self.ops[e]:
                    getattr(block, engobj[e])(run(e))
        nc.all_engine_barrier()
        for r in self.dres:
            r.sem = None
            r.cnt = 0


class Buf:
    __slots__ = ("t", "r")

    def __init__(self, t, name, const=False):
        self.t = t
        self.r = Res(name, const)


class K:
    def __init__(self, depth=DEPTH, dbg=False):
        self.depth = depth
        self.dbg = dbg
        nc = self.nc = bass.Bass("TRN2", target_bir_lowering=False)
        L = depth

        def din(name, shape):
            return nc.dram_tensor(name, list(shape), F32, kind="ExternalInput").ap()

        self.x_in = din("x", [NT, D])
        self.carry = din("carry", [128, 1])
        self.w_in = din("w_in", [L, D, INC])
        self.w_gate = din("w_gate", [L, D, 3 * D])
        self.w_branch = din("w_branch", [L, 3, BW, D])
        self.w_out = din("w_out", [L, D, D])
        self.norm_gc = din("norm_gc", [L, 128, 32])
        self.fin_g = din("fin_g", [D])
        self.b_gatec = din("b_gatec", [L, 128, 96])
        self.sgu_lng = din("sgu_lng", [L, BW])
        self.sgu_wT = din("sgu_wT", [L, 128, 8, 128])
        self.sgu_b = din("sgu_b", [L, 1024])
        self.gla_wgk = din("gla_wgk", [L, 2, 17, 1024])
        self.gla_ng = din("gla_ng", [L, 512])
        self.na_bias = din("na_bias", [L, 3, 6, 16, 128, 256])
        self.y_out = nc.dram_tensor("y", [NT, D], F32, kind="ExternalOutput").ap()

        def scr(name, shape, dt):
            okind = "ExternalOutput" if (dbg and (dbg is True or name in dbg)) else "Internal"
            return nc.dram_tensor(name, list(shape), dt, kind=okind).ap()

        self.xs = scr("xs", [NT, D], F32)
        self.hT = scr("hT", [32, 128, NT], BF16)
        self.fms = scr("fms", [NFM, 128, NT], BF16)
        self.tms = scr("tms", [NT, NTM], BF16)
        self.yT = scr("yT", [48, 128, NT], BF16)
        self.og = scr("og", [2, NT, BW], F32)
        self.mT = scr("mT", [32, 128, NT], BF16)
        self.w_in_b = [nc.dram_tensor(f"w_in_b{i}", [D, INC], BF16).ap() for i in range(2)]
        self.wg_s = [nc.dram_tensor(f"wg_s{i}", [32, 3, 128, 32 * 128], BF16).ap() for i in range(2)]
        self.wb_s = [nc.dram_tensor(f"wb_s{i}", [32, 3, 128, 16 * 128], BF16).ap() for i in range(2)]
        self.w_out_b = [nc.dram_tensor(f"w_out_b{i}", [D, D], BF16).ap() for i in range(2)]
        self.nb_b = [nc.dram_tensor(f"nb_b{i}", [30, 16, 128, 256], BF16).ap() for i in range(2)]
        self.pid = 0

    def begin(self, name):
        self.pid += 1
        self.ph = Phase(self.nc, f"{name}{self.pid}")
        self._bn = 0
        return self.ph

    def sb(self, shape, dt, name=None, const=False):
        self._bn += 1
        nm = f"{self.ph.name}_{name or 'b'}{self._bn}"
        return Buf(self.nc.alloc_sbuf_tensor(nm, list(shape), dt), nm, const)

    def ps(self, shape=(128, 512), dt=F32, name=None):
        self._bn += 1
        nm = f"{self.ph.name}_{name or 'ps'}{self._bn}"
        return Buf(self.nc.alloc_psum_tensor(nm, list(shape), dt), nm)

    def consts(self, ph, need_ident=True):
        idf = self.sb([128, 128], F32, "idf")
        idb = self.sb([128, 128], BF16, "idb")
        ph.op("pool", lambda e: e.memset(idf.t[:], 0.0), writes=[idf.r])
        ph.op("pool", lambda e: e.affine_select(out=idf.t[:], in_=idf.t[:], pattern=[[-1, 128]],
                                                 compare_op=ALU.not_equal, fill=1.0, base=0,
                                                 channel_multiplier=1),
              reads=[idf.r], writes=[idf.r])
        ph.op("dve", lambda e: e.tensor_copy(out=idb.t[:], in_=idf.t[:]), reads=[idf.r], writes=[idb.r])
        idb.r.const = True
        return idf, idb

    def conv(self, ph, l, parts):
        if l >= self.depth:
            return
        q = l % 2
        k = Res("cv")
        if "in" in parts:
            for i in range(32):
                ph.dma("pool", self.w_in_b[q][i * 128:(i + 1) * 128, :], self.w_in[l, i * 128:(i + 1) * 128, :], k)
        if "gb" in parts:
            wg = self.w_gate[l].rearrange("(k p) (b f c) -> f b p k c", p=128, b=3, f=32)
            wb = self.w_branch[l].rearrange("b (k p) (f c) -> f b p k c", p=128, f=32)
            for f in range(32):
                for b in range(3):
                    ph.dma("pool", self.wg_s[q][f, b].rearrange("p (k c) -> p k c", k=32), wg[f, b], k)
                    ph.dma("pool", self.wb_s[q][f, b].rearrange("p (k c) -> p k c", k=16), wb[f, b], k)
        if "out" in parts:
            for i in range(16):
                ph.dma("pool", self.w_out_b[q][i * 256:(i + 1) * 256, :], self.w_out[l, i * 256:(i + 1) * 256, :], k)
            nb = self.na_bias[l].rearrange("a t h p q -> (a t) h p q")
            for i in range(18):
                ph.dma("pool", self.nb_b[q][i], nb[i], k)
            car = self.sb([128, 2], F32, "car", const=True)
            ph.dma("sp", car.t[:, 0:1], self.carry, car.r, writes=[car.r])
            ph.op("dve", lambda e: e.tensor_scalar(out=car.t[:, 1:2], in0=car.t[:, 0:1], scalar1=-1.0, scalar2=1.0,
                                                   op0=ALU.mult, op1=ALU.add), reads=[car.r], writes=[car.r])
            a = self.sb([128, 16, 256], F32, "ti")
            b_ = self.sb([128, 16, 256], F32, "te")
            o = self.sb([128, 16, 256], BF16, "to")
            for t in range(6):
                for (edge, slot) in ((2, 18), (0, 24)):
                    ph.dma("sp", a.t[:], self.na_bias[l, 1, t].rearrange("h p q -> p h q"), a.r, writes=[a.r])
                    ph.dma("sp", b_.t[:], self.na_bias[l, edge, t].rearrange("h p q -> p h q"), b_.r, writes=[b_.r])
                    ph.op("dve", lambda e: e.tensor_scalar(out=a.t[:], in0=a.t[:], scalar1=car.t[:, 0:1],
                                                           scalar2=None, op0=ALU.mult),
                          reads=[a.r, car.r], writes=[a.r])
                    ph.op("dve", lambda e: e.scalar_tensor_tensor(
                        out=o.t[:], in0=b_.t[:], scalar=car.t[:, 1:2], in1=a.t[:], op0=ALU.mult, op1=ALU.add),
                        reads=[a.r, b_.r, car.r], writes=[o.r])
                    ph.dma("pool", self.nb_b[q][slot + t].rearrange("h p q -> p h q"), o.t[:], o.r, reads=[o.r])

    def phase_convert(self, l):
        nc = self.nc
        with nc.cleanup_on_exit():
            ph = self.begin("cv")
            self.conv(ph, l, ("in", "gb", "out"))
            ph.emit()

    def phase_norm(self, l, final=False):
        nc = self.nc
        src = self.x_in if (l == 0 and not final) else self.xs
        with nc.cleanup_on_exit():
            ph = self.begin("nm")
            if not final:
                idf, idb = self.consts(ph)
                gcol = self.sb([128, 32], F32, "gcol", const=True)
                ph.dma("sp", gcol.t[:], self.norm_gc[l], gcol.r, writes=[gcol.r])
                hts = [self.sb([128, 32, TT], BF16, "hts") for _ in range(2)]
                pst = [self.ps([128, 8, 128], BF16, "pst") for _ in range(4)]
            else:
                gbc = self.sb([128, D], F32, "gbc", const=True)
                ph.dma("sp", gbc.t[:], self.fin_g.partition_broadcast(128), gbc.r, writes=[gbc.r])
                yts = [self.sb([128, D], F32, "yt") for _ in range(2)]
            xts = [self.sb([128, D], F32, "xt") for _ in range(2)]
            junk = self.sb([128, D], BF16, "junk")
            xns = [self.sb([128, D], BF16, "xn") for _ in range(2)]
            sts = [self.sb([128, 4], F32, "st") for _ in range(2)]
            for it in range(NT // 128):
                xt, xn, st = xts[it % 2], xns[it % 2], sts[it % 2]
                ph.dma("sp", xt.t[:], src[it * 128:(it + 1) * 128, :], xt.r, writes=[xt.r])
                if l == 0 and not final:
                    ph.dma("pool", self.xs[it * 128:(it + 1) * 128, :], xt.t[:], xt.r, reads=[xt.r])
                ph.op("act", lambda e, xt=xt, st=st: e.activation(out=junk.t[:], in_=xt.t[:], func=AF.Square,
                                                                    accum_out=st.t[:, 0:1]),
                      reads=[xt.r], writes=[junk.r, st.r])
                ph.op("dve", lambda e, st=st: e.tensor_scalar(out=st.t[:, 1:2], in0=st.t[:, 0:1], scalar1=1.0 / D,
                                                               scalar2=EPS, op0=ALU.mult, op1=ALU.add),
                      reads=[st.r], writes=[st.r])
                ph.op("act", lambda e, st=st: e.activation(out=st.t[:, 2:3], in_=st.t[:, 1:2], func=AF.Sqrt),
                      reads=[st.r], writes=[st.r])
                ph.op("dve", lambda e, st=st: e.reciprocal(out=st.t[:, 3:4], in_=st.t[:, 2:3]),
                      reads=[st.r], writes=[st.r])
                if final:
                    yt = yts[it % 2]
                    ph.op("dve", lambda e, xt=xt, st=st, yt=yt: e.scalar_tensor_tensor(
                        out=yt.t[:], in0=xt.t[:], scalar=st.t[:, 3:4], in1=gbc.t[:], op0=ALU.mult, op1=ALU.mult),
                        reads=[xt.r, st.r, gbc.r], writes=[yt.r])
                    ph.dma("pool", self.y_out[it * 128:(it + 1) * 128, :], yt.t[:], yt.r, reads=[yt.r])
                    continue
                ph.op("dve", lambda e, xt=xt, st=st, xn=xn: e.tensor_scalar(
                    out=xn.t[:], in0=xt.t[:], scalar1=st.t[:, 3:4], scalar2=None, op0=ALU.mult),
                    reads=[xt.r, st.r], writes=[xn.r])
                ht = hts[(it // 4) % 2]
                sub = it % 4
                for k8 in range(4):
                    pt = pst[k8]

                    def tr(e, xn=xn, pt=pt, k8=k8):
                        for j in range(8):
                            kk = k8 * 8 + j
                            ins = e.transpose(pt.t[:, j, :], xn.t[:, kk * 128:(kk + 1) * 128], idb.t[:])
                        return ins
                    ph.op("pe", tr, reads=[xn.r, idb.r], writes=[pt.r])

                    def ev(e, pt=pt, ht=ht, k8=k8, sub=sub):
                        for j in range(8):
                            kk = k8 * 8 + j
                            ins = e.tensor_scalar(out=ht.t[:, kk, sub * 128:(sub + 1) * 128], in0=pt.t[:, j, :],
                                                  scalar1=gcol.t[:, kk:kk + 1], scalar2=None, op0=ALU.mult)
                        return ins
                    ph.op("dve", ev, reads=[pt.r, gcol.r], writes=[ht.r])
                if sub == 3:
                    tt = it // 4
                    ph.dma("pool", self.hT[:, :, tt * TT:(tt + 1) * TT].rearrange("k p t -> p k t"), ht.t[:],
                           ht.r, reads=[ht.r])
            ph.emit()

    def phase_inproj(self, l):
        nc = self.nc
        jobs = []

        def add(c0, n, mode, epi, dest, scale=1.0):
            for b in range(0, n, 512):
                w = min(512, n - b)
                d = dest + (b // 128 if mode == "FM" else b)
                jobs.append((c0 + b, w, [(mode, epi, scale, d)]))
        add(0, 2048, "FM", "copy", FM_AU)
        add(2048, 2048, "TM", "copy", TM_AV)
        add(4096, 2048, "FM", "silu", FM_ASZ)
        add(6144, 1024, "FM", "scale", FM_BQ, 256 ** -0.5)
        for b in range(0, 1024, 512):
            jobs.append((7168 + b, 512, [("FM", "copy", 1.0, FM_BK + b // 128), ("TM", "copy", 1.0, TM_BK + b)]))
        add(8192, 2048, "TM", "copy", TM_BV)
        add(10240, 2048, "TM", "silu", TM_BSZ)
        add(12288, 32, "FM", "copy", FM_LR)
        add(12320, 2048, "FM", "scale", FM_CQ, 128 ** -0.5)
        add(14368, 2048, "FM", "copy", FM_CK)
        add(16416, 2048, "TM", "copy", TM_CV)
        add(18464, 2048, "FM", "silu", FM_CSZ)
        with nc.cleanup_on_exit():
            ph = self.begin("ip")
            hts = [self.sb([128, 32, TT], BF16, "ht") for _ in range(2)]
            wts = [self.sb([128, 32, 512], BF16, "wt") for _ in range(3)]
            pss = [self.ps(name="ps") for _ in range(6)]
            stg = [self.sb([128, 512], BF16, "stg") for _ in range(6)]
            wv = self.w_in_b[l % 2].rearrange("(k p) n -> p k n", p=128)
            nj = 0
            ne = 0
            for tt in range(NTT):
                ht = hts[tt % 2]
                ph.dma("sp", ht.t[:], self.hT[:, :, tt * TT:(tt + 1) * TT].rearrange("k p t -> p k t"), ht.r,
                       writes=[ht.r])
                for (c0, w, subs) in jobs:
                    wt = wts[nj % 3]
                    nj += 1
                    ph.dma("sp", wt.t[:, :, 0:w], wv[:, :, c0:c0 + w], wt.r, writes=[wt.r])
                    for (mode, epi, scale, dest) in subs:
                        nsub = (w + 127) // 128 if mode == "FM" else TT // 128
                        for s in range(nsub):
                            p = pss[ne % 6]
                            sg = stg[ne % 6]
                            ne += 1
                            if mode == "FM":
                                m = min(128, w - s * 128)

                                def mm(e, p=p, wt=wt, ht=ht, s=s, m=m):
                                    for k in range(32):
                                        ins = e.matmul(p.t[0:m, :], lhsT=wt.t[:, k, s * 128:s * 128 + m],
                                                       rhs=ht.t[:, k, :], start=(k == 0), stop=(k == 31))
                                    return ins
                                pv, sv = p.t[0:m, :], sg.t[0:m, :]
                                dst = self.fms[dest + s, 0:m, tt * TT:(tt + 1) * TT]
                            else:
                                def mm(e, p=p, wt=wt, ht=ht, s=s, w=w):
                                    for k in range(32):
                                        ins = e.matmul(p.t[:, 0:w], lhsT=ht.t[:, k, s * 128:(s + 1) * 128],
                                                       rhs=wt.t[:, k, 0:w], start=(k == 0), stop=(k == 31))
                                    return ins
                                pv, sv = p.t[:, 0:w], sg.t[:, 0:w]
                                dst = self.tms[tt * TT + s * 128:tt * TT + (s + 1) * 128, dest:dest + w]
                            ph.op("pe", mm, reads=[wt.r, ht.r], writes=[p.r])
                            if epi == "silu":
                                ph.op("act", lambda e, pv=pv, sv=sv: e.activation(out=sv, in_=pv, func=AF.Silu),
                                      reads=[p.r], writes=[sg.r])
                            elif epi == "scale":
                                ph.op("dve", lambda e, pv=pv, sv=sv, scale=scale: e.tensor_scalar(
                                    out=sv, in0=pv, scalar1=scale, scalar2=None, op0=ALU.mult),
                                    reads=[p.r], writes=[sg.r])
                            elif ne % 4 == 0:
                                ph.op("act", lambda e, pv=pv, sv=sv: e.activation(out=sv, in_=pv, func=AF.Copy),
                                      reads=[p.r], writes=[sg.r])
                            else:
                                ph.op("dve", lambda e, pv=pv, sv=sv: e.tensor_copy(out=sv, in_=pv),
                                      reads=[p.r], writes=[sg.r])
                            ph.dma("pool", dst, sv, sg.r, reads=[sg.r])
            ph.emit()

    def phase_sgu(self, l):
        nc = self.nc
        with nc.cleanup_on_exit():
            ph = self.begin("sg")
            lng = self.sb([128, BW], F32, "lng", const=True)
            ph.dma("sp", lng.t[:], self.sgu_lng[l].partition_broadcast(128), lng.r, writes=[lng.r])
            wsf = self.sb([128, 8, 128], F32, "wsf")
            ph.dma("sp", wsf.t[:], self.sgu_wT[l], wsf.r, writes=[wsf.r])
            wsb = self.sb([128, 8, 128], BF16, "wsb", const=True)
            ph.op("dve", lambda e: e.tensor_copy(out=wsb.t[:], in_=wsf.t[:]), reads=[wsf.r], writes=[wsb.r])
            bsb = self.sb([128, 8, 128], F32, "bsb", const=True)
            ph.dma("sp", bsb.t[:], self.sgu_b[l].partition_broadcast(128).rearrange("p (g i) -> p g i", g=8),
                   bsb.r, writes=[bsb.r])
            bs4 = self.sb([128, 8, 4, 128], F32, "bs4", const=True)

            def cpb(e):
                for c in range(4):
                    ins = e.tensor_copy(out=bs4.t[:, :, c, :], in_=bsb.t[:])
                return ins
            ph.op("dve", cpb, reads=[bsb.r], writes=[bs4.r])
            vts = [self.sb([128, 4, BW], BF16, "vt") for _ in range(2)]
            uts = [self.sb([128, 16, TT], BF16, "ut") for _ in range(2)]
            zts = [self.sb([128, 16, TT], BF16, "zt") for _ in range(2)]
            vn = self.sb([128, BW], F32, "vn")
            vnb = [self.sb([128, 4, BW], BF16, "vnb") for _ in range(1)]
            junk = self.sb([128, BW], BF16, "junk")
            sts = [self.sb([128, 8], F32, "st") for _ in range(2)]
            pss = [self.ps([128, 4, 128], F32, "ps") for _ in range(4)]
            t1s = [self.sb([128, 4, 128], F32, "t1") for _ in range(2)]
            yst = [self.sb([128, 16, TT], BF16, "ys") for _ in range(2)]
            nst = 0
            for tt in range(NTT):
                vt, ut, zt, vb, ys = vts[tt % 2], uts[tt % 2], zts[tt % 2], vnb[0], yst[tt % 2]
                t0 = tt * TT
                ph.dma("sp", vt.t[:], self.tms[t0:t0 + TT, TM_AV:TM_AV + BW].rearrange("(c p) n -> p c n", p=128),
                       vt.r, writes=[vt.r])
                ph.dma("sp", ut.t[:], self.fms[FM_AU:FM_AU + 16, :, t0:t0 + TT].rearrange("k p t -> p k t"),
                       ut.r, writes=[ut.r])
                ph.dma("sp", zt.t[:], self.fms[FM_ASZ:FM_ASZ + 16, :, t0:t0 + TT].rearrange("k p t -> p k t"),
                       zt.r, writes=[zt.r])
                for c in range(4):
                    st = sts[nst % 2]
                    nst += 1
                    vc = vt.t[:, c, :]
                    ph.op("dve", lambda e, vc=vc, st=st: e.reduce_sum(out=st.t[:, 0:1], in_=vc, axis=AX.X),
                          reads=[vt.r], writes=[st.r])
                    ph.op("act", lambda e, vc=vc, st=st: e.activation(out=junk.t[:], in_=vc, func=AF.Square,
                                                                       accum_out=st.t[:, 1:2]),
                          reads=[vt.r], writes=[junk.r, st.r])
                    ph.op("dve", lambda e, st=st: e.tensor_scalar(out=st.t[:, 2:3], in0=st.t[:, 0:1],
                                                                   scalar1=1.0 / BW, scalar2=None, op0=ALU.mult),
                          reads=[st.r], writes=[st.r])
                    ph.op("dve", lambda e, st=st: e.tensor_tensor(out=st.t[:, 3:4], in0=st.t[:, 2:3],
                                                                   in1=st.t[:, 2:3], op=ALU.mult),
                          reads=[st.r], writes=[st.r])
                    ph.op("dve", lambda e, st=st: e.scalar_tensor_tensor(
                        out=st.t[:, 4:5], in0=st.t[:, 1:2], scalar=1.0 / BW, in1=st.t[:, 3:4],
                        op0=ALU.mult, op1=ALU.subtract), reads=[st.r], writes=[st.r])
                    ph.op("dve", lambda e, st=st: e.tensor_scalar(out=st.t[:, 4:5], in0=st.t[:, 4:5], scalar1=EPS,
                                                                   scalar2=None, op0=ALU.add),
                          reads=[st.r], writes=[st.r])
                    ph.op("act", lambda e, st=st: e.activation(out=st.t[:, 5:6], in_=st.t[:, 4:5], func=AF.Sqrt),
                          reads=[st.r], writes=[st.r])
                    ph.op("dve", lambda e, st=st: e.reciprocal(out=st.t[:, 6:7], in_=st.t[:, 5:6]),
                          reads=[st.r], writes=[st.r])
                    ph.op("dve", lambda e, st=st: e.scalar_tensor_tensor(
                        out=st.t[:, 7:8], in0=st.t[:, 2:3], scalar=-1.0, in1=st.t[:, 6:7],
                        op0=ALU.mult, op1=ALU.mult), reads=[st.r], writes=[st.r])
                    ph.op("act", lambda e, vc=vc, st=st: e.activation(out=vn.t[:], in_=vc, func=AF.Identity,
                                                                       bias=st.t[:, 7:8], scale=st.t[:, 6:7]),
                          reads=[vt.r, st.r], writes=[vn.r])
                    ph.op("dve", lambda e, vb=vb, c=c: e.tensor_tensor(out=vb.t[:, c, :], in0=vn.t[:],
                                                                        in1=lng.t[:], op=ALU.mult),
                          reads=[vn.r, lng.r], writes=[vb.r])
                for cc in range(16):
                    g = cc // 2
                    p = pss[cc % 4]
                    t1 = t1s[cc % 2]

                    def mm(e, p=p, vb=vb, cc=cc, g=g):
                        for c in range(4):
                            ins = e.matmul(p.t[:, c, :], lhsT=vb.t[:, c, cc * 128:(cc + 1) * 128],
                                           rhs=wsb.t[:, g, :], start=True, stop=True)
                        return ins
                    ph.op("pe", mm, reads=[vb.r, wsb.r], writes=[p.r])
                    ph.op("dve", lambda e, p=p, t1=t1, g=g: e.tensor_tensor(
                        out=t1.t[:], in0=p.t[:], in1=bs4.t[:, g, :, :], op=ALU.add),
                        reads=[p.r, bs4.r], writes=[t1.r])
                    t1f = t1.t[:].rearrange("p c i -> p (c i)")
                    ph.op("dve", lambda e, t1f=t1f, ut=ut, cc=cc: e.tensor_tensor(
                        out=t1f, in0=t1f, in1=ut.t[:, cc, :], op=ALU.mult), reads=[t1.r, ut.r], writes=[t1.r])
                    ph.op("dve", lambda e, t1f=t1f, zt=zt, ys=ys, cc=cc: e.tensor_tensor(
                        out=ys.t[:, cc, :], in0=t1f, in1=zt.t[:, cc, :], op=ALU.mult),
                        reads=[t1.r, zt.r], writes=[ys.r])
                ph.dma("pool", self.yT[0:16, :, t0:t0 + TT].rearrange("k p t -> p k t"), ys.t[:], ys.r,
                       reads=[ys.r])
            ph.emit()

    def phase_gla(self, l, rev):
        nc = self.nc
        GT_ = 256
        ntile = NT // GT_
        d = 1 if rev else 0
        with nc.cleanup_on_exit():
            ph = self.begin("gb" if rev else "gf")
            if not rev:
                self.conv(ph, l + 1, ("in",))
            triA = self.sb([64, 64], F32, "triA")
            triB = self.sb([64, 64], F32, "triB")
            mskA = self.sb([64, 4, 64], F32, "mskA")
            sA = 1 if not rev else -1
            ph.op("pool", lambda e: e.memset(triA.t[:], -1.0 / 16), writes=[triA.r])
            ph.op("pool", lambda e: e.affine_select(out=triA.t[:], in_=triA.t[:], pattern=[[sA, 64]],
                                                     compare_op=ALU.is_ge, fill=0.0, base=0, channel_multiplier=-sA),
                  reads=[triA.r], writes=[triA.r])
            ph.op("pool", lambda e: e.memset(triB.t[:], -1.0 / 16), writes=[triB.r])
            ph.op("pool", lambda e: e.affine_select(out=triB.t[:], in_=triB.t[:], pattern=[[-sA, 64]],
                                                     compare_op=ALU.is_gt, fill=0.0, base=0, channel_multiplier=sA),
                  reads=[triB.r], writes=[triB.r])
            ph.op("pool", lambda e: e.memset(mskA.t[:], 1.0), writes=[mskA.r])
            ph.op("pool", lambda e: e.affine_select(out=mskA.t[:], in_=mskA.t[:], pattern=[[0, 4], [sA, 64]],
                                                     compare_op=ALU.is_ge, fill=0.0, base=0, channel_multiplier=-sA),
                  reads=[mskA.r], writes=[mskA.r])
            triA.r.const = triB.r.const = mskA.r.const = True
            wgf = self.sb([17, 1024], F32, "wgf")
            ph.dma("sp", wgf.t[:], self.gla_wgk[l, d], wgf.r, writes=[wgf.r])
            wgb = self.sb([17, 1024], BF16, "wgb", const=True)
            ph.op("dve", lambda e: e.tensor_copy(out=wgb.t[:], in_=wgf.t[:]), reads=[wgf.r], writes=[wgb.r])
            lrt = self.sb([17, NT], BF16, "lrt", const=True)
            ph.op("pool", lambda e: e.memset(lrt.t[:], 1.0), writes=[lrt.r])
            ph.dma("sp", lrt.t[0:16, :], self.fms[FM_LR, d * 16:(d + 1) * 16, :], lrt.r, reads=[lrt.r],
                   writes=[lrt.r])
            car = self.sb([128, 1], F32, "car", const=True)
            ph.dma("sp", car.t[:], self.carry, car.r, writes=[car.r])
            S8 = [self.sb([128, 512], F32, "S") for _ in range(8)]
            Sb2 = [[self.sb([128, 512], BF16, "Sb") for _ in range(8)] for _ in range(2)]
            for dc in range(8):
                ph.op("pool", lambda e, dc=dc: e.memset(S8[dc].t[:], 0.0), writes=[S8[dc].r])
                ph.op("pool", lambda e, dc=dc: e.memset(Sb2[0][dc].t[:], 0.0), writes=[Sb2[0][dc].r])
            qts = [self.sb([128, 8, GT_], BF16, "qt") for _ in range(2)]
            kts = [self.sb([128, 8, GT_], BF16, "kt") for _ in range(2)]
            kms = [self.sb([64, 4, 1024], BF16, "km") for _ in range(2)]
            vms = [self.sb([64, 4, BW], BF16, "vm") for _ in range(2)]
            ex2 = [self.sb([64, 1024], F32, "ex") for _ in range(2)]
            Lg2 = [self.sb([64, 1024], F32, "Lg") for _ in range(2)]
            Ep2 = [self.sb([128, 8, 64], F32, "Ep") for _ in range(2)]
            Em2 = [self.sb([128, 8, 64], F32, "Em") for _ in range(2)]
            Er2 = [self.sb([64, 1024], F32, "Er") for _ in range(2)]
            qe2 = [self.sb([128, 8, 64], BF16, "qe") for _ in range(2)]
            ke2 = [self.sb([128, 8, 64], BF16, "ke") for _ in range(2)]
            kk2 = [self.sb([64, 1024], BF16, "kk") for _ in range(2)]
            aTm2 = [self.sb([64, 4, 64], BF16, "aTm") for _ in range(2)]
            nsb = [0]
            ceng = ["act", "pool"]
            ost = [self.sb([64, BW], F32, "ost") for _ in range(2)]
            pX = [self.ps(name="pX") for _ in range(2)]
            pG = self.ps([128, 8, 64], F32, "pG")
            pA = self.ps([64, 4, 64], F32, "pA")
            pO = [self.ps(name="pO") for _ in range(2)]
            pU = [self.ps(name="pU") for _ in range(2)]
            fms, tms = self.fms, self.tms
            order = list(range(ntile))
            if rev:
                order = order[::-1]
            nch = 0
            for ti, tile in enumerate(order):
                t0 = tile * GT_
                qt, kt, km, vm = qts[ti % 2], kts[ti % 2], kms[ti % 2], vms[ti % 2]
                ph.dma("sp", qt.t[:], fms[FM_BQ:FM_BQ + 8, :, t0:t0 + GT_].rearrange("k p t -> p k t"), qt.r,
                       writes=[qt.r])
                ph.dma("sp", kt.t[:], fms[FM_BK:FM_BK + 8, :, t0:t0 + GT_].rearrange("k p t -> p k t"), kt.r,
                       writes=[kt.r])
                ph.dma("sp", km.t[:], tms[t0:t0 + GT_, TM_BK:TM_BK + 1024].rearrange("(c p) n -> p c n", p=64),
                       km.r, writes=[km.r])
                ph.dma("sp", vm.t[:], tms[t0:t0 + GT_, TM_BV:TM_BV + BW].rearrange("(c p) n -> p c n", p=64),
                       vm.r, writes=[vm.r])
                cs = [3, 2, 1, 0] if rev else [0, 1, 2, 3]
                for c in cs:
                    tk0 = t0 + c * 64
                    par = nch % 2
                    ex, Lg, Ep, Em, Er = ex2[par], Lg2[par], Ep2[par], Em2[par], Er2[par]
                    qe, ke, kk, aTm = qe2[par], ke2[par], kk2[par], aTm2[par]
                    if (not rev and tk0 == NT // 2) or (rev and tk0 == NT // 2 - 64):
                        for dc in range(8):
                            sbc = Sb2[nsb[0] % 2][dc]
                            ph.op("dve", lambda e, dc=dc: e.tensor_scalar(
                                out=S8[dc].t[:], in0=S8[dc].t[:], scalar1=car.t[:, 0:1], scalar2=None, op0=ALU.mult),
                                reads=[S8[dc].r, car.r], writes=[S8[dc].r])
                            if dc % 2 == 0:
                                ph.op("act", lambda e, dc=dc, sbc=sbc: e.activation(out=sbc.t[:], in_=S8[dc].t[:],
                                                                                   func=AF.Copy),
                                      reads=[S8[dc].r], writes=[sbc.r])
                            else:
                                ph.op("pool", lambda e, dc=dc, sbc=sbc: e.tensor_copy(out=sbc.t[:], in_=S8[dc].t[:]),
                                      reads=[S8[dc].r], writes=[sbc.r])

                    def mm1(e, tk0=tk0):
                        for hf in range(2):
                            ins = e.matmul(pX[hf].t[0:64, :], lhsT=lrt.t[0:17, tk0:tk0 + 64],
                                           rhs=wgb.t[0:17, hf * 512:(hf + 1) * 512], start=True, stop=True)
                        return ins
                    ph.op("pe", mm1, reads=[lrt.r, wgb.r], writes=[pX[0].r, pX[1].r])

                    def a1(e, ex=ex):
                        for hf in range(2):
                            ins = e.activation(out=ex.t[:, hf * 512:(hf + 1) * 512], in_=pX[hf].t[0:64, :],
                                               func=AF.Exp, scale=-1.0)
                        return ins
                    ph.op("act", a1, reads=[pX[0].r, pX[1].r], writes=[ex.r])
                    ph.op("act", lambda e, Lg=Lg, ex=ex: e.activation(out=Lg.t[:], in_=ex.t[:], func=AF.Ln, bias=1.0),
                          reads=[ex.r], writes=[Lg.r])

                    def mm2(e, Lg=Lg):
                        for dc in range(8):
                            ins = e.matmul(pG.t[:, dc, :], lhsT=Lg.t[:, dc * 128:(dc + 1) * 128], rhs=triA.t[:],
                                           start=True, stop=True)
                        for hf in range(2):
                            ins = e.matmul(pX[hf].t[0:64, :], lhsT=triB.t[:], rhs=Lg.t[:, hf * 512:(hf + 1) * 512],
                                           start=True, stop=True)
                        return ins
                    ph.op("pe", mm2, reads=[Lg.r, triA.r, triB.r, ex.r], writes=[pG.r, pX[0].r, pX[1].r])
                    ph.op("act", lambda e, Ep=Ep: e.activation(out=Ep.t[:], in_=pG.t[:], func=AF.Exp),
                          reads=[pG.r], writes=[Ep.r])
                    ph.op("act", lambda e, Em=Em: e.activation(out=Em.t[:], in_=pG.t[:], func=AF.Exp, scale=-1.0),
                          reads=[pG.r], writes=[Em.r])

                    def a3(e, Er=Er):
                        for hf in range(2):
                            ins = e.activation(out=Er.t[:, hf * 512:(hf + 1) * 512], in_=pX[hf].t[0:64, :],
                                               func=AF.Exp)
                        return ins
                    ph.op("act", a3, reads=[pX[0].r, pX[1].r], writes=[Er.r])
                    ph.op("dve", lambda e, qt=qt, c=c, qe=qe, Ep=Ep: e.tensor_tensor(
                        out=qe.t[:], in0=qt.t[:, :, c * 64:(c + 1) * 64], in1=Ep.t[:], op=ALU.mult),
                        reads=[qt.r, Ep.r], writes=[qe.r])
                    ph.op("dve", lambda e, kt=kt, c=c, ke=ke, Em=Em: e.tensor_tensor(
                        out=ke.t[:], in0=kt.t[:, :, c * 64:(c + 1) * 64], in1=Em.t[:], op=ALU.mult),
                        reads=[kt.r, Em.r], writes=[ke.r])
                    ph.op("dve", lambda e, km=km, c=c, kk=kk, Er=Er: e.tensor_tensor(
                        out=kk.t[:], in0=km.t[:, c, :], in1=Er.t[:], op=ALU.mult),
                        reads=[km.r, Er.r], writes=[kk.r])

                    def mm3(e, ke=ke, qe=qe):
                        for h in range(4):
                            for q in range(2):
                                dc = 2 * h + q
                                ins = e.matmul(pA.t[:, h, :], lhsT=ke.t[:, dc, :], rhs=qe.t[:, dc, :],
                                               start=(q == 0), stop=(q == 1))
                        return ins
                    ph.op("pe", mm3, reads=[ke.r, qe.r], writes=[pA.r])
                    ph.op("dve", lambda e, aTm=aTm: e.tensor_tensor(out=aTm.t[:], in0=pA.t[:], in1=mskA.t[:], op=ALU.mult),
                          reads=[pA.r, mskA.r], writes=[aTm.r])
                    gl = 0 if rev else 63
                    sbw = Sb2[(nsb[0] + 1) % 2]
                    for dc in range(8):
                        pu = pU[dc % 2]
                        h = dc // 2

                        def mm5(e, dc=dc, pu=pu, h=h, vm=vm, c=c, kk=kk):
                            return e.matmul(pu.t[:], lhsT=kk.t[:, dc * 128:(dc + 1) * 128],
                                            rhs=vm.t[:, c, h * 512:(h + 1) * 512], start=True, stop=True)
                        ph.op("pe", mm5, reads=[kk.r, vm.r], writes=[pu.r])
                        ph.op("dve", lambda e, dc=dc, pu=pu, gl=gl, Ep=Ep: e.scalar_tensor_tensor(
                            out=S8[dc].t[:], in0=S8[dc].t[:], scalar=Ep.t[:, dc, gl:gl + 1], in1=pu.t[:],
                            op0=ALU.mult, op1=ALU.add), reads=[S8[dc].r, Ep.r, pu.r], writes=[S8[dc].r])
                        if dc % 2 == 0:
                            ph.op("act", lambda e, dc=dc, sbw=sbw: e.activation(out=sbw[dc].t[:], in_=S8[dc].t[:],
                                                                               func=AF.Copy),
                                  reads=[S8[dc].r], writes=[sbw[dc].r])
                        else:
                            ph.op("pool", lambda e, dc=dc, sbw=sbw: e.tensor_copy(out=sbw[dc].t[:], in_=S8[dc].t[:]),
                                  reads=[S8[dc].r], writes=[sbw[dc].r])
                    o_s = ost[nch % 2]
                    nch += 1
                    sbr = Sb2[nsb[0] % 2]
                    for h in range(4):
                        po = pO[h % 2]

                        def mm4(e, h=h, po=po, vm=vm, c=c, aTm=aTm, qe=qe, sbr=sbr):
                            e.matmul(po.t[0:64, :], lhsT=aTm.t[:, h, :], rhs=vm.t[:, c, h * 512:(h + 1) * 512],
                                     start=True, stop=False)
                            e.matmul(po.t[0:64, :], lhsT=qe.t[:, 2 * h, :], rhs=sbr[2 * h].t[:],
                                     start=False, stop=False)
                            return e.matmul(po.t[0:64, :], lhsT=qe.t[:, 2 * h + 1, :], rhs=sbr[2 * h + 1].t[:],
                                            start=False, stop=True)
                        ph.op("pe", mm4, reads=[aTm.r, vm.r, qe.r, sbr[2 * h].r, sbr[2 * h + 1].r], writes=[po.r])
                        ph.op("act" if h % 2 == 0 else "dve", (lambda e, h=h, po=po, o_s=o_s: e.activation(
                            out=o_s.t[:, h * 512:(h + 1) * 512], in_=po.t[0:64, :], func=AF.Copy)) if h % 2 == 0 else
                            (lambda e, h=h, po=po, o_s=o_s: e.tensor_copy(
                                out=o_s.t[:, h * 512:(h + 1) * 512], in_=po.t[0:64, :])),
                            reads=[po.r], writes=[o_s.r])
                    ph.dma("pool", self.og[d, tk0:tk0 + 64, :], o_s.t[:], o_s.r, reads=[o_s.r])
                    nsb[0] += 1
            ph.emit()

    def phase_gla_out(self, l):
        nc = self.nc
        with nc.cleanup_on_exit():
            ph = self.begin("go")
            self.conv(ph, l + 1, ("out",))
            idf, idb = self.consts(ph)
            ngb = self.sb([128, 512], F32, "ngb", const=True)
            ph.dma("sp", ngb.t[:], self.gla_ng[l].partition_broadcast(128), ngb.r, writes=[ngb.r])
            ofs = [self.sb([128, BW], F32, "of") for _ in range(2)]
            obs = [self.sb([128, BW], F32, "ob") for _ in range(2)]
            szs = [self.sb([128, BW], BF16, "sz") for _ in range(2)]
            junk = self.sb([128, 512], BF16, "junk")
            sts = [self.sb([128, 16], F32, "st") for _ in range(2)]
            yb = [self.sb([128, BW], BF16, "yb") for _ in range(2)]
            pst = [self.ps([128, 8, 128], BF16, "pst") for _ in range(2)]
            yts = [self.sb([128, 16, TT], BF16, "yt") for _ in range(2)]
            for it in range(NT // 128):
                of, ob, sz, st, y = ofs[it % 2], obs[it % 2], szs[it % 2], sts[it % 2], yb[it % 2]
                r0 = it * 128
                ph.dma("sp", of.t[:], self.og[0, r0:r0 + 128, :], of.r, writes=[of.r])
                ph.dma("sp", ob.t[:], self.og[1, r0:r0 + 128, :], ob.r, writes=[ob.r])
                ph.dma("sp", sz.t[:], self.tms[r0:r0 + 128, TM_BSZ:TM_BSZ + BW], sz.r, writes=[sz.r])
                ph.op("dve", lambda e, of=of, ob=ob: e.tensor_tensor(out=of.t[:], in0=of.t[:], in1=ob.t[:],
                                                                     op=ALU.add),
                      reads=[of.r, ob.r], writes=[of.r])

                def sq(e, of=of, st=st):
                    for h in range(4):
                        ins = e.activation(out=junk.t[:], in_=of.t[:, h * 512:(h + 1) * 512], func=AF.Square,
                                           accum_out=st.t[:, h:h + 1])
                    return ins
                ph.op("act", sq, reads=[of.r], writes=[junk.r, st.r])
                ph.op("dve", lambda e, st=st: e.tensor_scalar(out=st.t[:, 4:8], in0=st.t[:, 0:4], scalar1=1.0 / 512,
                                                               scalar2=EPS, op0=ALU.mult, op1=ALU.add),
                      reads=[st.r], writes=[st.r])
                ph.op("act", lambda e, st=st: e.activation(out=st.t[:, 8:12], in_=st.t[:, 4:8], func=AF.Sqrt),
                      reads=[st.r], writes=[st.r])
                ph.op("dve", lambda e, st=st: e.reciprocal(out=st.t[:, 12:16], in_=st.t[:, 8:12]),
                      reads=[st.r], writes=[st.r])

                def nrm(e, of=of, st=st):
                    for h in range(4):
                        ins = e.scalar_tensor_tensor(out=of.t[:, h * 512:(h + 1) * 512],
                                                     in0=of.t[:, h * 512:(h + 1) * 512],
                                                     scalar=st.t[:, 12 + h:13 + h], in1=ngb.t[:],
                                                     op0=ALU.mult, op1=ALU.mult)
                    return ins
                ph.op("dve", nrm, reads=[of.r, st.r, ngb.r], writes=[of.r])
                ph.op("dve", lambda e, of=of, sz=sz, y=y: e.tensor_tensor(out=y.t[:], in0=of.t[:], in1=sz.t[:],
                                                                           op=ALU.mult),
                      reads=[of.r, sz.r], writes=[y.r])
                yt = yts[(it // 4) % 2]
                sub = it % 4
                for k8 in range(2):
                    pt = pst[k8]

                    def tr(e, y=y, pt=pt, k8=k8):
                        for j in range(8):
                            kk_ = k8 * 8 + j
                            ins = e.transpose(pt.t[:, j, :], y.t[:, kk_ * 128:(kk_ + 1) * 128], idb.t[:])
                        return ins
                    ph.op("pe", tr, reads=[y.r, idb.r], writes=[pt.r])
                    ph.op("act", lambda e, pt=pt, yt=yt, k8=k8, sub=sub: e.activation(
                        out=yt.t[:, k8 * 8:(k8 + 1) * 8, sub * 128:(sub + 1) * 128], in_=pt.t[:], func=AF.Copy),
                        reads=[pt.r], writes=[yt.r])
                if sub == 3:
                    t0 = (it // 4) * TT
                    ph.dma("pool", self.yT[16:32, :, t0:t0 + TT].rearrange("k p t -> p k t"), yt.t[:], yt.r,
                           reads=[yt.r])
            ph.emit()

    def phase_na(self, l):
        nc = self.nc
        NB = NT // 256
        with nc.cleanup_on_exit():
            ph = self.begin("na")
            self.conv(ph, l + 1, ("gb",))
            idf, idb = self.consts(ph)
            ones = self.sb([128, 128], BF16, "ones", const=True)
            ph.op("pool", lambda e: e.memset(ones.t[:], 1.0), writes=[ones.r])
            qts = [self.sb([128, NT], BF16, "q") for _ in range(2)]
            kts = [self.sb([128, NT], BF16, "k") for _ in range(2)]
            vts = [self.sb([128, 32, 128], BF16, "v") for _ in range(2)]
            zts = [self.sb([128, NT], BF16, "z") for _ in range(2)]
            bts = [self.sb([128, 30, 256], BF16, "b") for _ in range(2)]
            yts = [self.sb([128, NT], BF16, "y") for _ in range(2)]
            pS = [self.ps([128, 256], F32, "pS") for _ in range(3)]
            pO = [self.ps([128, 256], F32, "pO") for _ in range(2)]
            pD = [self.ps([128, 256], F32, "pD") for _ in range(2)]
            pts = [self.sb([128, 256], BF16, "pt") for _ in range(3)]
            rd = [self.sb([128, 256], F32, "rd") for _ in range(2)]
            o1 = [self.sb([128, 256], F32, "o1") for _ in range(2)]
            nS = 0
            nB = 0
            for h in range(16):
                qt, kt, vt, zt, bt, yt = qts[h % 2], kts[h % 2], vts[h % 2], zts[h % 2], bts[h % 2], yts[h % 2]
                ph.dma("sp", qt.t[:], self.fms[FM_CQ + h], qt.r, writes=[qt.r])
                ph.dma("sp", kt.t[:], self.fms[FM_CK + h], kt.r, writes=[kt.r])
                ph.dma("sp", zt.t[:], self.fms[FM_CSZ + h], zt.r, writes=[zt.r])
                ph.dma("sp", vt.t[:], self.tms[:, TM_CV + h * 128:TM_CV + (h + 1) * 128].rearrange(
                    "(c p) n -> p c n", p=128), vt.r, writes=[vt.r])
                ph.dma("sp", bt.t[:], self.nb_b[l % 2][:, h].rearrange("a p q -> p a q"), bt.r, writes=[bt.r])
                for b in range(NB):
                    kind = {0: 0, NB - 1: 2, NB // 2 - 1: 3, NB // 2: 4}.get(b, 1)
                    tl = [t for t in range(6) if 0 <= 4 * b - 4 + 2 * t and 4 * b - 4 + 2 * t + 2 <= NT // 64]
                    po, pd = pO[nB % 2], pD[nB % 2]
                    q0 = b * 256
                    slots = []
                    for t in tl:
                        slots.append((pS[nS % 3], pts[nS % 3]))
                        nS += 1

                    def do_mms(ix):
                        t = tl[ix]
                        kc = (4 * b - 4 + 2 * t) // 2
                        p_s = slots[ix][0]

                        def mms(e, p_s=p_s, kt=kt, qt=qt, bt=bt, kc=kc, q0=q0, kind=kind, t=t):
                            e.matmul(p_s.t[:], lhsT=kt.t[:, kc * 128:(kc + 1) * 128], rhs=qt.t[:, q0:q0 + 256],
                                     start=True, stop=False)
                            return e.matmul(p_s.t[:], lhsT=idb.t[:], rhs=bt.t[:, kind * 6 + t, :],
                                            start=False, stop=True)
                        ph.op("pe", mms, reads=[kt.r, qt.r, bt.r, idb.r], writes=[p_s.r])

                    def do_rest(ix):
                        t = tl[ix]
                        kc = (4 * b - 4 + 2 * t) // 2
                        p_s, pt = slots[ix]
                        ph.op("act", lambda e, p_s=p_s, pt=pt: e.activation(out=pt.t[:], in_=p_s.t[:], func=AF.Exp),
                              reads=[p_s.r], writes=[pt.r])

                        def mmo(e, po=po, pd=pd, vt=vt, pt=pt, kc=kc, first=(ix == 0), last=(ix == len(tl) - 1)):
                            e.matmul(po.t[:], lhsT=vt.t[:, kc, :], rhs=pt.t[:], start=first, stop=last)
                            return e.matmul(pd.t[:], lhsT=ones.t[:], rhs=pt.t[:], start=first, stop=last)
                        ph.op("pe", mmo, reads=[vt.r, pt.r, ones.r], writes=[po.r, pd.r])
                    do_mms(0)
                    for ix in range(len(tl)):
                        if ix + 1 < len(tl):
                            do_mms(ix + 1)
                        do_rest(ix)
                    r_, o_ = rd[nB % 2], o1[nB % 2]
                    nB += 1
                    ph.op("dve", lambda e, r_=r_, pd=pd: e.reciprocal(out=r_.t[:], in_=pd.t[:]),
                          reads=[pd.r], writes=[r_.r])
                    ph.op("dve", lambda e, o_=o_, po=po, r_=r_: e.tensor_tensor(out=o_.t[:], in0=po.t[:],
                                                                                   in1=r_.t[:], op=ALU.mult),
                          reads=[po.r, r_.r], writes=[o_.r])
                    ph.op("dve", lambda e, o_=o_, zt=zt, yt=yt, q0=q0: e.tensor_tensor(
                        out=yt.t[:, q0:q0 + 256], in0=o_.t[:], in1=zt.t[:, q0:q0 + 256], op=ALU.mult),
                        reads=[o_.r, zt.r], writes=[yt.r])
                ph.dma("pool", self.yT[32 + h], yt.t[:], yt.r, reads=[yt.r])
            ph.emit()

    def phase_merge(self, l):
        nc = self.nc
        with nc.cleanup_on_exit():
            ph = self.begin("mg")
            bg = self.sb([128, 96], F32, "bg", const=True)
            ph.dma("sp", bg.t[:], self.b_gatec[l], bg.r, writes=[bg.r])
            ht = self.sb([128, 32, TT], BF16, "ht")
            yt = self.sb([128, 48, TT], BF16, "yt")
            wgs = [self.sb([128, 32, 128], BF16, "wg") for _ in range(3)]
            wbs = [self.sb([128, 16, 128], BF16, "wb") for _ in range(3)]
            pG = [self.ps(name="pG") for _ in range(4)]
            pB = [self.ps(name="pB") for _ in range(4)]
            gs = [self.sb([128, TT], F32, "g") for _ in range(4)]
            ms = [self.sb([128, TT], F32, "m") for _ in range(2)]
            mts = [self.sb([128, 32, TT], BF16, "mt") for _ in range(1)]
            n3 = 0
            nf = 0
            for tt in range(NTT):
                t0 = tt * TT
                mt = mts[0]
                ph.dma("sp", ht.t[:], self.hT[:, :, t0:t0 + TT].rearrange("k p t -> p k t"), ht.r, writes=[ht.r])
                ph.dma("sp", yt.t[:], self.yT[:, :, t0:t0 + TT].rearrange("k p t -> p k t"), yt.r, writes=[yt.r])
                for f in range(32):
                    m = ms[nf % 2]
                    nf += 1
                    for b in range(3):
                        pg, pb, g = pG[n3 % 4], pB[n3 % 4], gs[n3 % 4]
                        wg, wb = wgs[n3 % 3], wbs[n3 % 3]
                        n3 += 1
                        ph.dma("sp", wg.t[:], self.wg_s[l % 2][f, b].rearrange("p (k c) -> p k c", k=32), wg.r,
                               writes=[wg.r])
                        ph.dma("sp", wb.t[:], self.wb_s[l % 2][f, b].rearrange("p (k c) -> p k c", k=16), wb.r,
                               writes=[wb.r])

                        def mmg(e, pg=pg, wg=wg, b=b):
                            for k in range(32):
                                ins = e.matmul(pg.t[:], lhsT=wg.t[:, k, :], rhs=ht.t[:, k, :],
                                               start=(k == 0), stop=(k == 31))
                            return ins
                        ph.op("pe", mmg, reads=[wg.r, ht.r], writes=[pg.r])

                        def mmb(e, pb=pb, wb=wb, b=b):
                            for k in range(16):
                                ins = e.matmul(pb.t[:], lhsT=wb.t[:, k, :], rhs=yt.t[:, b * 16 + k, :],
                                               start=(k == 0), stop=(k == 15))
                            return ins
                        ph.op("pe", mmb, reads=[wb.r, yt.r], writes=[pb.r])
                        col = b * 32 + f
                        ph.op("act", lambda e, pg=pg, g=g, col=col: e.activation(
                            out=g.t[:], in_=pg.t[:], func=AF.Sigmoid, bias=bg.t[:, col:col + 1]),
                            reads=[pg.r, bg.r], writes=[g.r])
                        if b == 0:
                            ph.op("dve", lambda e, m=m, g=g, pb=pb: e.tensor_tensor(out=m.t[:], in0=g.t[:],
                                                                                     in1=pb.t[:], op=ALU.mult),
                                  reads=[g.r, pb.r], writes=[m.r])
                        else:
                            ph.op("dve", lambda e, g=g, pb=pb: e.tensor_tensor(out=g.t[:], in0=g.t[:], in1=pb.t[:],
                                                                                op=ALU.mult),
                                  reads=[g.r, pb.r], writes=[g.r])
                            if b == 1:
                                ph.op("pool", lambda e, m=m, g=g: e.tensor_tensor(out=m.t[:], in0=m.t[:],
                                                                                   in1=g.t[:], op=ALU.add),
                                      reads=[m.r, g.r], writes=[m.r])
                            else:
                                ph.op("pool", lambda e, m=m, g=g, mt=mt, f=f: e.tensor_tensor(
                                    out=mt.t[:, f, :], in0=m.t[:], in1=g.t[:], op=ALU.add),
                                    reads=[m.r, g.r], writes=[mt.r])
                ph.dma("pool", self.mT[:, :, t0:t0 + TT].rearrange("k p t -> p k t"), mt.t[:], mt.r, reads=[mt.r])
            ph.emit()

    def phase_out(self, l):
        nc = self.nc
        with nc.cleanup_on_exit():
            ph = self.begin("op")
            mts = [self.sb([128, 32, TT], BF16, "mt") for _ in range(2)]
            wos = [self.sb([128, 32, 512], BF16, "wo") for _ in range(2)]
            pss = [self.ps(name="ps") for _ in range(4)]
            xss = [self.sb([128, 512], F32, "xs") for _ in range(4)]
            wv = self.w_out_b[l % 2].rearrange("(k p) n -> p k n", p=128)
            nw = 0
            ne = 0
            for tt in range(NTT):
                t0 = tt * TT
                mt = mts[tt % 2]
                ph.dma("sp", mt.t[:], self.mT[:, :, t0:t0 + TT].rearrange("k p t -> p k t"), mt.r, writes=[mt.r])
                for n in range(8):
                    wo = wos[nw % 2]
                    nw += 1
                    ph.dma("sp", wo.t[:], wv[:, :, n * 512:(n + 1) * 512], wo.r, writes=[wo.r])
                    for s in range(4):
                        p, xs_ = pss[ne % 4], xss[ne % 4]
                        ne += 1
                        rows = slice(t0 + s * 128, t0 + (s + 1) * 128)
                        cols = slice(n * 512, (n + 1) * 512)
                        ph.dma("sp", xs_.t[:], self.xs[rows, cols], xs_.r, writes=[xs_.r])

                        def mm(e, p=p, mt=mt, wo=wo, s=s):
                            for k in range(32):
                                ins = e.matmul(p.t[:], lhsT=mt.t[:, k, s * 128:(s + 1) * 128], rhs=wo.t[:, k, :],
                                               start=(k == 0), stop=(k == 31))
                            return ins
                        ph.op("pe", mm, reads=[mt.r, wo.r], writes=[p.r])
                        ph.op("dve", lambda e, p=p, xs_=xs_: e.tensor_tensor(out=xs_.t[:], in0=xs_.t[:], in1=p.t[:],
                                                                              op=ALU.add),
                              reads=[p.r, xs_.r], writes=[xs_.r])
                        ph.dma("pool", self.xs[rows, cols], xs_.t[:], xs_.r, reads=[xs_.r])
            ph.emit()

    def build(self, upto=None):
        self.phase_convert(0)
        for l in range(self.depth):
            self.phase_norm(l)
            self.phase_inproj(l)
            self.phase_sgu(l)
            self.phase_gla(l, False)
            self.phase_gla(l, True)
            self.phase_gla_out(l)
            self.phase_na(l)
            self.phase_merge(l)
            self.phase_out(l)
        self.phase_norm(self.depth, final=True)
        return self.nc


def _na_bias_tiles(rpb_l):
    H = rpb_l.shape[0]
    cols = np.arange(64)
    cs = np.clip(cols - 8, 0, 48)
    col_ok = (cols[None, :] >= cs[:, None]) & (cols[None, :] < cs[:, None] + 16)
    dc = np.clip(cols[None, :] - cols[:, None] + 15, 0, 30)
    out = np.full((3, 6, H, 128, 256), NEG, np.float32)
    for kind in range(3):
        for t in range(6):
            for kr2 in range(2):
                ko = -4 + 2 * t + kr2
                for qr in range(4):
                    krel = ko - qr
                    if kind == 1:
                        ok = -4 <= krel <= 3
                    elif kind == 0:
                        ok = 0 <= ko <= 7
                    else:
                        ok = -4 <= ko <= 3
                    if not ok:
                        continue
                    blk = rpb_l[:, krel + 7, :][:, dc]
                    blk = np.where(col_ok[None], blk, np.float32(NEG))
                    out[kind, t, :, kr2 * 64:(kr2 + 1) * 64, qr * 64:(qr + 1) * 64] = blk.transpose(0, 2, 1)
    return out


_NC_CACHE = {}


def _layout(inputs, depth=DEPTH):
    f32 = np.float32
    xp = np.asarray(inputs["x_prompt"], f32)
    xsmp = np.asarray(inputs["x_sample"], f32)
    slabs = [xsmp[0], xsmp[1],
             np.concatenate([xp[0], xp[1]], 0), np.concatenate([xp[2], xp[3]], 0)]
    zero = np.zeros_like(xp[0])
    for i in range(4, 8):
        slabs.append(np.concatenate([xp[i], zero], 0))
    two_seq = [False, False] + [True] * 6
    L = depth
    shared = {
        "w_in": np.ascontiguousarray(np.asarray(inputs["w_in"], f32)[:L]),
        "w_gate": np.ascontiguousarray(np.asarray(inputs["w_gate"], f32)[:L]),
        "w_branch": np.ascontiguousarray(np.asarray(inputs["w_branch"], f32)[:L]),
        "w_out": np.ascontiguousarray(np.asarray(inputs["w_out"], f32)[:L]),
        "norm_gc": np.ascontiguousarray(np.asarray(inputs["norm_g"], f32)[:L].reshape(L, 32, 128).transpose(0, 2, 1)),
        "fin_g": np.asarray(inputs["final_norm_g"], f32),
        "b_gatec": np.ascontiguousarray(np.asarray(inputs["b_gate"], f32)[:L].reshape(L, 96, 128).transpose(0, 2, 1)),
        "sgu_lng": np.ascontiguousarray(np.asarray(inputs["sgu_ln_g"], f32)[:L]),
        "sgu_wT": np.ascontiguousarray(np.asarray(inputs["sgu_w"], f32)[:L].transpose(0, 3, 1, 2)),
        "sgu_b": np.ascontiguousarray(np.asarray(inputs["sgu_b"], f32)[:L].reshape(L, 1024)),
        "gla_wgk": np.ascontiguousarray(np.concatenate(
            [np.asarray(inputs["gla_w_gk"], f32)[:L], np.asarray(inputs["gla_b_gk"], f32)[:L, :, None, :]], axis=2)),
        "gla_ng": np.ascontiguousarray(np.asarray(inputs["gla_norm_g"], f32)[:L]),
    }
    rpb = np.asarray(inputs["na_rpb"], f32)[:L]
    shared["na_bias"] = np.stack([_na_bias_tiles(rpb[l]) for l in range(L)])
    in_maps = []
    for c in range(8):
        m = dict(shared)
        m["x"] = np.ascontiguousarray(slabs[c])
        m["carry"] = np.full((128, 1), 0.0 if two_seq[c] else 1.0, f32)
        in_maps.append(m)
    return in_maps


def kernel(**inputs):
    if "nc" not in _NC_CACHE:
        _NC_CACHE["nc"] = K().build()
    nc = _NC_CACHE["nc"]
    in_maps = _layout(inputs)
    res = run_bass_kernel_spmd(nc, in_maps, core_ids=list(range(8)))
    ys = [np.asarray(r["y"]) for r in res.results]
    y_sample = np.stack([ys[0], ys[1]], 0)
    y_prompt = np.stack([ys[2][:2048], ys[2][2048:], ys[3][:2048], ys[3][2048:],
                         ys[4][:2048], ys[5][:2048], ys[6][:2048], ys[7][:2048]], 0)
    return (y_prompt.astype(np.float32), y_sample.astype(np.float32))
```
